# Optimizing a Trainium2 kernel written in Bass

```python
import math
import jax, jax.numpy as jnp
from jax import lax
import numpy as np

D_MODEL = 1024
BATCH = 8
SEQ = 4096
DEPTH = 1

HEAD_DIM = 128
N_HEADS_MOBA = 8
N_HEADS_GDN = 8
MOBA_WIDTH = N_HEADS_MOBA * HEAD_DIM
GDN_WIDTH = N_HEADS_GDN * HEAD_DIM
MOBA_BLOCK = 256
MOBA_TOPK = 3
MOBA_Q_CHUNK = 32
GDN_CHUNK = 64
CONV_WIDTH = 4
ROPE_THETA = 10000.0
D_FF = -(-8 * D_MODEL // (3 * 256)) * 256
PLE_DIM = 256
EPS = 1e-6

IN_SIZES = (MOBA_WIDTH, MOBA_WIDTH, MOBA_WIDTH, 3 * GDN_WIDTH, GDN_WIDTH,
            N_HEADS_GDN, N_HEADS_GDN, D_MODEL, D_MODEL)
IN_SPLITS = tuple(int(s) for s in np.cumsum(IN_SIZES)[:-1])
IN_WIDTH = int(sum(IN_SIZES))

kernel_name = "hybrid_moba_gated_deltanet_block"


def rms_norm(x, w):
    xf = x.astype(jnp.float32)
    y = xf * lax.rsqrt(jnp.mean(xf * xf, axis=-1, keepdims=True) + EPS)
    return (y * w.astype(jnp.float32)).astype(x.dtype)


def l2_norm(x):
    xf = x.astype(jnp.float32)
    return xf * lax.rsqrt(jnp.sum(xf * xf, axis=-1, keepdims=True) + EPS)


def rope_tables(seq):
    inv = 1.0 / (ROPE_THETA ** (jnp.arange(0, HEAD_DIM, 2, dtype=jnp.float32) / HEAD_DIM))
    ang = jnp.arange(seq, dtype=jnp.float32)[:, None] * inv[None, :]
    return jnp.cos(ang), jnp.sin(ang)


def apply_rope(x, cos, sin):
    x1, x2 = jnp.split(x.astype(jnp.float32), 2, axis=-1)
    out = jnp.concatenate([x1 * cos - x2 * sin, x2 * cos + x1 * sin], axis=-1)
    return out.astype(x.dtype)


def causal_depthwise_conv(x, w):
    c = x.shape[-1]
    return lax.conv_general_dilated(
        x, w[:, None, :].astype(x.dtype), window_strides=(1,),
        padding=[(CONV_WIDTH - 1, 0)], dimension_numbers=("NWC", "WIO", "NWC"),
        feature_group_count=c)


def moba_attention(q, k, v):
    b, h, s, hd = q.shape
    nb = -(-s // MOBA_BLOCK)
    pad = nb * MOBA_BLOCK - s
    n_sel = min(MOBA_TOPK, nb)
    scale = hd ** -0.5
    k_blocks = jnp.pad(k, ((0, 0), (0, 0), (0, pad), (0, 0))).reshape(b, h, nb, MOBA_BLOCK, hd)
    v_blocks = jnp.pad(v, ((0, 0), (0, 0), (0, pad), (0, 0))).reshape(b, h, nb, MOBA_BLOCK, hd)
    k_mean = jnp.mean(k_blocks.astype(jnp.float32), axis=3)
    blk_ids = jnp.arange(nb)
    n_chunks = s // MOBA_Q_CHUNK

    def one_chunk(c):
        start = c * MOBA_Q_CHUNK
        q_c = lax.dynamic_slice_in_dim(q, start, MOBA_Q_CHUNK, axis=2)
        q_blk = start // MOBA_BLOCK
        gate = jnp.einsum("bhcd,bhnd->bhcn", q_c.astype(jnp.float32), k_mean)
        gate = jnp.where(blk_ids < q_blk, gate, -jnp.inf)
        _, sel = lax.top_k(gate, n_sel)
        sel_valid = sel < q_blk
        idx = sel.reshape(b, h, MOBA_Q_CHUNK * n_sel, 1, 1)
        k_sel = jnp.take_along_axis(k_blocks, idx, axis=2).reshape(
            b, h, MOBA_Q_CHUNK, n_sel, MOBA_BLOCK, hd)
        v_sel = jnp.take_along_axis(v_blocks, idx, axis=2).reshape(
            b, h, MOBA_Q_CHUNK, n_sel, MOBA_BLOCK, hd)
        s_past = jnp.einsum("bhcd,bhcknd->bhckn", q_c, k_sel).astype(jnp.float32) * scale
        s_past = jnp.where(sel_valid[..., None], s_past, -jnp.inf)
        s_past = s_past.reshape(b, h, MOBA_Q_CHUNK, n_sel * MOBA_BLOCK)
        k_own = lax.dynamic_index_in_dim(k_blocks, q_blk, axis=2, keepdims=False)
        v_own = lax.dynamic_index_in_dim(v_blocks, q_blk, axis=2, keepdims=False)
        s_own = jnp.einsum("bhcd,bhnd->bhcn", q_c, k_own).astype(jnp.float32) * scale
        q_pos = start + jnp.arange(MOBA_Q_CHUNK)
        k_pos = q_blk * MOBA_BLOCK + jnp.arange(MOBA_BLOCK)
        s_own = jnp.where(k_pos[None, :] <= q_pos[:, None], s_own, -jnp.inf)
        probs = jax.nn.softmax(jnp.concatenate([s_past, s_own], axis=-1), axis=-1)
        p_past = probs[..., : n_sel * MOBA_BLOCK].reshape(
            b, h, MOBA_Q_CHUNK, n_sel, MOBA_BLOCK).astype(v.dtype)
        p_own = probs[..., n_sel * MOBA_BLOCK:].astype(v.dtype)
        return (jnp.einsum("bhckn,bhcknd->bhcd", p_past, v_sel)
                + jnp.einsum("bhcn,bhnd->bhcd", p_own, v_own))

    out = lax.map(one_chunk, jnp.arange(n_chunks))
    return out.transpose(1, 2, 0, 3, 4).reshape(b, h, s, hd)


def gated_delta_rule(q, k, v, g, beta):
    b, h, s, dk = q.shape
    dv = v.shape[-1]
    L = GDN_CHUNK
    nc = s // L
    f32 = jnp.float32
    q = q.astype(f32) * (dk ** -0.5)
    k = k.astype(f32)
    v = v.astype(f32)
    q = q.reshape(b, h, nc, L, dk)
    k = k.reshape(b, h, nc, L, dk)
    v = v.reshape(b, h, nc, L, dv)
    g = g.astype(f32).reshape(b, h, nc, L)
    beta = beta.astype(f32).reshape(b, h, nc, L)
    gc = jnp.cumsum(g, axis=-1)
    causal = jnp.tril(jnp.ones((L, L), dtype=bool))
    strict = jnp.tril(jnp.ones((L, L), dtype=bool), k=-1)
    decay = jnp.exp(jnp.where(causal, gc[..., :, None] - gc[..., None, :], -jnp.inf))
    k_beta = k * beta[..., None]
    a_low = jnp.where(strict, jnp.einsum("bhnid,bhnjd->bhnij", k_beta, k) * decay, 0.0)
    t_mat = a_low + jnp.eye(L, dtype=f32)
    rhs = jnp.concatenate([v * beta[..., None], k_beta * jnp.exp(gc)[..., None]], axis=-1)
    sol = lax.linalg.triangular_solve(t_mat, rhs, left_side=True, lower=True, unit_diagonal=True)
    u = sol[..., :dv]
    w = sol[..., dv:]
    attn = jnp.where(causal, jnp.einsum("bhnid,bhnjd->bhnij", q, k) * decay, 0.0)

    def step(state, inp):
        q_i, k_i, u_i, w_i, g_i, a_i = inp
        v_new = u_i - jnp.einsum("bhld,bhde->bhle", w_i, state)
        o = (jnp.einsum("bhld,bhde->bhle", q_i * jnp.exp(g_i)[..., None], state)
             + jnp.einsum("bhlm,bhme->bhle", a_i, v_new))
        g_last = g_i[..., -1]
        state = (state * jnp.exp(g_last)[..., None, None]
                 + jnp.einsum("bhld,bhle->bhde", k_i * jnp.exp(g_last[..., None] - g_i)[..., None], v_new))
        return state, o

    xs = tuple(jnp.moveaxis(t, 2, 0) for t in (q, k, u, w, gc, attn))
    state0 = jnp.zeros((b, h, dk, dv), dtype=f32)
    _, o = lax.scan(step, state0, xs)
    return jnp.moveaxis(o, 0, 2).reshape(b, h, s, dv)


def setup_inputs(seed: int = 0) -> dict:
    key = jax.random.key(seed)
    ks = jax.random.split(key, 24)
    f32 = jnp.float32

    def nrm(k, shape, scale):
        return jax.random.normal(k, shape, f32) * scale

    def gain(k, n):
        return 1.0 + 0.1 * jax.random.normal(k, (DEPTH, n), f32)

    dt = jnp.exp(jax.random.uniform(ks[7], (DEPTH, N_HEADS_GDN), f32,
                                    math.log(1e-3), math.log(1e-1)))
    return {
        "x": nrm(ks[0], (BATCH, SEQ, D_MODEL), 1.0),
        "p": nrm(ks[1], (DEPTH, BATCH, SEQ, PLE_DIM), 1.0),
        "ln_pre_mix": gain(ks[2], D_MODEL),
        "w_in": nrm(ks[3], (DEPTH, D_MODEL, IN_WIDTH), D_MODEL ** -0.5),
        "conv_w": nrm(ks[4], (DEPTH, CONV_WIDTH, 3 * GDN_WIDTH), CONV_WIDTH ** -0.5),
        "a_log": jnp.log(jax.random.uniform(ks[5], (DEPTH, N_HEADS_GDN), f32, 1.0, 16.0)),
        "dt_bias": jnp.log(jnp.expm1(dt)),
        "gdn_norm": gain(ks[6], HEAD_DIM),
        "w_proj_a": nrm(ks[8], (DEPTH, MOBA_WIDTH, D_MODEL), MOBA_WIDTH ** -0.5),
        "w_proj_b": nrm(ks[9], (DEPTH, GDN_WIDTH, D_MODEL), GDN_WIDTH ** -0.5),
        "w_out": nrm(ks[10], (DEPTH, D_MODEL, D_MODEL), D_MODEL ** -0.5),
        "ln_post_mix": gain(ks[11], D_MODEL),
        "ln_pre_ffn": gain(ks[12], D_MODEL),
        "w_ffn_gate": nrm(ks[13], (DEPTH, D_MODEL, D_FF), D_MODEL ** -0.5),
        "w_ffn_up": nrm(ks[14], (DEPTH, D_MODEL, D_FF), D_MODEL ** -0.5),
        "w_ffn_down": nrm(ks[15], (DEPTH, D_FF, D_MODEL), D_FF ** -0.5),
        "ln_post_ffn": gain(ks[16], D_MODEL),
        "w_ple": nrm(ks[17], (DEPTH, PLE_DIM, D_MODEL), PLE_DIM ** -0.5),
        "ln_ple": gain(ks[18], D_MODEL),
        "w_ple_gate": nrm(ks[19], (DEPTH, D_MODEL, D_MODEL), D_MODEL ** -0.5),
    }


def reference(x, p, ln_pre_mix, w_in, conv_w, a_log, dt_bias, gdn_norm, w_proj_a, w_proj_b,
              w_out, ln_post_mix, ln_pre_ffn, w_ffn_gate, w_ffn_up, w_ffn_down, ln_post_ffn,
              w_ple, ln_ple, w_ple_gate):
    b, s, _ = x.shape
    cos, sin = rope_tables(s)

    def to_heads(t, n):
        return t.reshape(b, s, n, HEAD_DIM).transpose(0, 2, 1, 3)

    def from_heads(t):
        return t.transpose(0, 2, 1, 3).reshape(b, s, -1)

    h = x
    for i in range(DEPTH):
        u = rms_norm(h, ln_pre_mix[i])
        proj = u @ w_in[i]
        q_a, k_a, v_a, qkv_b, z_b, b_b, a_b, gate_a, gate_b = jnp.split(proj, IN_SPLITS, axis=-1)

        qa = apply_rope(to_heads(q_a, N_HEADS_MOBA), cos, sin)
        ka = apply_rope(to_heads(k_a, N_HEADS_MOBA), cos, sin)
        va = to_heads(v_a, N_HEADS_MOBA)
        y_a = from_heads(moba_attention(qa, ka, va))

        qkv_b = jax.nn.silu(causal_depthwise_conv(qkv_b, conv_w[i]))
        q_b, k_b, v_b = jnp.split(qkv_b, 3, axis=-1)
        qb = l2_norm(to_heads(q_b, N_HEADS_GDN))
        kb = l2_norm(to_heads(k_b, N_HEADS_GDN))
        vb = to_heads(v_b, N_HEADS_GDN)
        g_log = -jnp.exp(a_log[i].astype(jnp.float32)) * jax.nn.softplus(
            a_b.astype(jnp.float32) + dt_bias[i].astype(jnp.float32))
        beta = jax.nn.sigmoid(b_b.astype(jnp.float32))
        o_b = gated_delta_rule(qb, kb, vb, g_log.transpose(0, 2, 1), beta.transpose(0, 2, 1))
        o_b = o_b.transpose(0, 2, 1, 3).astype(h.dtype)
        z = z_b.reshape(b, s, N_HEADS_GDN, HEAD_DIM)
        y_b = (rms_norm(o_b, gdn_norm[i]) * jax.nn.silu(z)).reshape(b, s, GDN_WIDTH)

        merged = (jax.nn.sigmoid(gate_a) * (y_a @ w_proj_a[i])
                  + jax.nn.sigmoid(gate_b) * (y_b @ w_proj_b[i]))
        h = h + rms_norm(merged @ w_out[i], ln_post_mix[i])

        f = rms_norm(h, ln_pre_ffn[i])
        f = (jax.nn.silu(f @ w_ffn_gate[i]) * (f @ w_ffn_up[i])) @ w_ffn_down[i]
        h = h + rms_norm(f, ln_post_ffn[i])

        e = rms_norm(p[i].astype(h.dtype) @ w_ple[i], ln_ple[i])
        h = h + jax.nn.sigmoid(h @ w_ple_gate[i]) * e
    return h
```

```python
import numpy as np
from contextlib import ExitStack
import concourse.bass as bass
import concourse.mybir as mybir
from concourse.bass_utils import run_bass_kernel_spmd

F32 = mybir.dt.float32
BF16 = mybir.dt.bfloat16
ALU = mybir.AluOpType
AF = mybir.ActivationFunctionType
AX = mybir.AxisListType

D = 1024
HD = 128
NH = 8
DFF = 2816
PLE = 256
INW = 9232
EPS = 1e-6
NEG = -30000.0


class Prog:
    ENG = ("pe", "act", "dve", "pool", "sp")

    def __init__(self, nc, es, n_dma_sems=32):
        self.nc = nc
        self.ops = {e: [] for e in self.ENG}
        self.lastw = {}
        self.readers = {}
        self.sem = {e: es.enter_context(nc.semaphore("s_" + e)) for e in ("pe", "act", "dve", "pool")}
        self.dsems = [es.enter_context(nc.semaphore(f"dq{i}")) for i in range(n_dma_sems)]
        self.dcount = [0] * n_dma_sems
        self.dlast = [None] * n_dma_sems
        self.ndma = 0
        self.npool = 0

    @staticmethod
    def _is_psum(k):
        return isinstance(k, tuple) and k[0] == "ps"

    def _deps(self, eng, r, w, is_dma):
        strong = set()
        weak = set()
        for k in r:
            t = self.lastw.get(k)
            if t is not None:
                strong.add(t)
            if self._is_psum(k):
                for t2 in self.readers.get(k, ()):
                    weak.add(t2)
        for k in w:
            t = self.lastw.get(k)
            if t is not None:
                strong.add(t)
            for t2 in self.readers.get(k, ()):
                weak.add(t2)
        deps = set()
        for t in strong:
            if t[0] == "c" and t[1] == eng and eng == "pe":
                continue
            deps.add(t)
        for t in weak:
            if t[0] == "c" and t[1] == eng and not is_dma:
                continue
            deps.add(t)
        return deps

    def _mark(self, deps):
        done = getattr(self, "done", None)
        for t in deps:
            if t[0] == "c" and (done is None or t[2] >= done[t[1]]):
                self.ops[t[1]][t[2]]["signal"] = True

    def _commit(self, tok, r, w):
        for k in w:
            self.lastw[k] = tok
            self.readers[k] = []
        for k in r:
            self.readers.setdefault(k, []).append(tok)

    def op(self, eng, fn, r=(), w=()):
        deps = self._deps(eng, r, w, False)
        self._mark(deps)
        idx = len(self.ops[eng])
        self.ops[eng].append(dict(fn=fn, deps=deps, signal=False, dma=None))
        tok = ("c", eng, idx)
        self._commit(tok, r, w)
        return tok

    def dma(self, q, out, in_, r=(), w=(), **kw):
        deps = self._deps(q, r, w, True)
        if q == "pool":
            s = self.npool % 8
            self.npool += 1
        else:
            s = 8 + self.ndma % (len(self.dsems) - 8)
            self.ndma += 1
        if self.dlast[s] is not None:
            deps.add(self.dlast[s])
        self._mark(deps)
        self.dcount[s] += 1
        tok = ("d", s, 16 * self.dcount[s])
        self.dlast[s] = tok
        fn = lambda e, out=out, in_=in_, kw=kw: e.dma_start(out=out, in_=in_, **kw)
        self.ops[q].append(dict(fn=fn, deps=deps, signal=False, dma=s))
        self._commit(tok, r, w)
        return tok

    def finish(self, q="sp"):
        deps = set(t for t in self.dlast if t is not None)
        self.ops[q].append(dict(fn=None, deps=deps, signal=False, dma=None))

    def emit(self):
        nc = self.nc
        self.finish()
        if not hasattr(self, "done"):
            self.done = {e: 0 for e in self.ENG}
            self.sigcnt = {e: [] for e in self.ENG}
            self.waited = {e: {} for e in self.ENG}
        for e in ("pe", "act", "dve", "pool"):
            for o in reversed(self.ops[e][self.done[e]:]):
                if o["fn"] is not None and o["dma"] is None:
                    o["signal"] = True
                    break
        for e in ("pe", "act", "dve", "pool"):
            arr = self.sigcnt[e]
            c = arr[-1] if arr else 0
            for o in self.ops[e][self.done[e]:]:
                if o["signal"] and o["dma"] is None and o["fn"] is not None:
                    c += 1
                arr.append(c)

        def sigval(e, idx):
            ops = self.ops[e]
            j = idx
            while not (ops[j]["signal"] and ops[j]["dma"] is None and ops[j]["fn"] is not None):
                j += 1
            return self.sigcnt[e][j]

        def run(ename, eng):
            waited = self.waited[ename]
            for o in self.ops[ename][self.done[ename]:]:
                need = {}
                for t in o["deps"]:
                    if t[0] == "c":
                        key = ("c", t[1])
                        val = sigval(t[1], t[2])
                    else:
                        key = ("d", t[1])
                        val = t[2]
                    if need.get(key, 0) < val:
                        need[key] = val
                for key, val in need.items():
                    if waited.get(key, 0) >= val:
                        continue
                    sh = self.sem[key[1]] if key[0] == "c" else self.dsems[key[1]]
                    eng.wait_ge(sh, val)
                    waited[key] = val
                if o["fn"] is None:
                    continue
                ins = o["fn"](eng)
                if o["dma"] is not None:
                    ins.then_inc(self.dsems[o["dma"]], 16)
                elif o["signal"]:
                    ins.then_inc(self.sem[ename], 1)

        with nc.allow_non_contiguous_dma(reason="small strided constant loads"), nc.Block() as block:
            @block.tensor
            def _(e):
                run("pe", e)

            @block.scalar
            def _(e):
                run("act", e)

            @block.vector
            def _(e):
                run("dve", e)

            @block.gpsimd
            def _(e):
                run("pool", e)

            @block.sync
            def _(e):
                run("sp", e)
        for e in self.ENG:
            self.done[e] = len(self.ops[e])


def bcast_ap(ap, dims):
    return bass.AP(ap.tensor, ap.offset, [list(ap.ap[0])] + [list(d) for d in dims])


def host_consts(S):
    import ml_dtypes
    bf = ml_dtypes.bfloat16
    c = {}
    c["ident"] = np.eye(128, dtype=np.float32)
    inv = (1.0 / (np.float32(10000.0) ** (np.arange(0, HD, 2, dtype=np.float32) / np.float32(HD)))).astype(np.float32)
    ang = (np.arange(S, dtype=np.float32)[:, None] * inv[None, :]).astype(np.float32)
    cos = np.cos(ang).astype(np.float32).T
    sin = np.sin(ang).astype(np.float32).T
    c["ropecos"] = np.ascontiguousarray(np.concatenate([cos, cos], 0))
    c["ropesin"] = np.ascontiguousarray(np.concatenate([-sin, sin], 0))
    key = np.arange(128)[:, None, None] + 128 * np.arange(2)[None, :, None]
    qq = np.arange(256)[None, None, :]
    c["caus"] = np.where(key <= qq, 0.0, NEG).astype(bf)
    sel = np.zeros((128, 16, 128), np.float32)
    for kb in range(16):
        sel[kb, kb, :] = 1.0
    c["sel"] = sel.astype(bf)
    inval = np.zeros((128, 16, 16), np.float32)
    for qb in range(16):
        inval[:, qb, qb:] = -1e30
    c["inval"] = inval
    p = np.arange(128)
    same = (p[:, None] // 64) == (p[None, :] // 64)
    c["gM1"] = (same & (p[:, None] <= p[None, :])).astype(np.float32)
    c["gM2"] = (same & (p[:, None] > p[None, :])).astype(np.float32)
    mc = np.zeros((128, 2, 128), np.float32)
    mc[0:64, 0, :] = 1.0
    mc[64:128, 1, :] = 1.0
    c["gMc"] = mc
    c["gcausneg"] = np.where(same & (p[None, :] <= p[:, None]), 0.0, NEG).astype(bf)
    c["gstrict"] = (same & (p[None, :] < p[:, None])).astype(np.float32).astype(bf)
    rm = np.zeros((128, 2), np.float32)
    rm[0:64, 0] = 1.0
    rm[64:128, 1] = 1.0
    c["growmask"] = rm
    return c


def build(S, debug=(), heads=tuple(range(NH)), phases=("p1", "p2", "p3", "p4")):
    T = S // 128
    NG = S // 512
    NB = S // 256
    nc = bass.Bass("TRN2", target_bir_lowering=False)
    es = ExitStack()

    def dram_in(name, shape, dt=F32):
        return nc.dram_tensor(name, list(shape), dt, kind="ExternalInput").ap()

    x = dram_in("x", [S, D])
    ln_pre_mix = dram_in("ln_pre_mix", [D])
    w_in = dram_in("w_in", [D, INW])
    ident_in = dram_in("ident", [128, 128])
    ropecos = dram_in("ropecos", [128, S])
    ropesin = dram_in("ropesin", [128, S])
    caus_in = dram_in("caus", [128, 2, 256], BF16)
    sel_in = dram_in("sel", [128, 16, 128], BF16)
    inval_in = dram_in("inval", [128, 16, 16])
    conv_w = dram_in("conv_w", [4, 3072])
    a_log = dram_in("a_log", [NH])
    dt_bias = dram_in("dt_bias", [NH])
    gdn_norm = dram_in("gdn_norm", [HD])
    p_in = dram_in("p", [S, PLE])
    w_proj_a = dram_in("w_proj_a", [D, D])
    w_proj_b = dram_in("w_proj_b", [D, D])
    w_out = dram_in("w_out", [D, D])
    ln_post_mix = dram_in("ln_post_mix", [D])
    ln_pre_ffn = dram_in("ln_pre_ffn", [D])
    w_ffn_gate = dram_in("w_ffn_gate", [D, DFF])
    w_ffn_up = dram_in("w_ffn_up", [D, DFF])
    w_ffn_down = dram_in("w_ffn_down", [DFF, D])
    ln_post_ffn = dram_in("ln_post_ffn", [D])
    w_ple = dram_in("w_ple", [PLE, D])
    ln_ple = dram_in("ln_ple", [D])
    w_ple_gate = dram_in("w_ple_gate", [D, D])
    h1_scr = nc.dram_tensor("h1_scr", [S, D], F32, kind="ExternalOutput" if "h1" in debug else "Internal").ap()
    e_scr = nc.dram_tensor("e_scr", [S, D], F32, kind="Internal").ap()
    gM1_in = dram_in("gM1", [128, 128])
    gM2_in = dram_in("gM2", [128, 128])
    gMc_in = dram_in("gMc", [128, 2, 128])
    gcausneg_in = dram_in("gcausneg", [128, 128], BF16)
    gstrict_in = dram_in("gstrict", [128, 128], BF16)
    growmask_in = dram_in("growmask", [128, 2])
    out = nc.dram_tensor("out", [S, D], F32, kind="ExternalOutput").ap()
    uT_scr = nc.dram_tensor("uT_scr", [128, 8, S], BF16, kind="Internal").ap()
    ynT_scr = nc.dram_tensor("ynT_scr", [128, 8, S], BF16,
                             kind="ExternalOutput" if "ynT" in debug else "Internal").ap()
    yaT_scr = nc.dram_tensor("yaT_scr", [NH, 128, S], BF16,
                             kind="ExternalOutput" if "yaT" in debug else "Internal").ap()
    dbg = {}
    if "uT" in debug:
        dbg["uT"] = nc.dram_tensor("dbg_uT", [128, 8, S], BF16, kind="ExternalOutput").ap()

    def sb(name, shape, dt=F32):
        return es.enter_context(nc.sbuf_tensor(name, list(shape), dt))

    def ps(name):
        return es.enter_context(nc.psum_tensor(name, [128, 512], F32))

    P = Prog(nc, es)
    banks = [ps(f"bank{i}") for i in range(8)]
    w_in_v = w_in.rearrange("(c p) n -> p c n", p=128)

    ident_f = sb("ident_f", [128, 128])
    ident_b = sb("ident_b", [128, 128], BF16)
    gainT = sb("gainT", [128, 8])
    epsc = sb("epsc", [128, 1])
    es12 = ExitStack()
    uT = es12.enter_context(nc.sbuf_tensor("uT", [128, 8, S], BF16))
    P.dma("sp", ident_f[:], ident_in, w=["ident_f"])
    P.dma("sp", gainT[:], ln_pre_mix.rearrange("(c p) -> p c", p=128), w=["gainT"])
    P.op("dve", lambda e: e.tensor_copy(out=ident_b[:], in_=ident_f[:]), r=["ident_f"], w=["ident_b"])
    P.op("pool", lambda e: e.memset(epsc[:], EPS), w=["epsc"])

    with ExitStack() as ph:
        def sbp(name, shape, dt=F32):
            return ph.enter_context(nc.sbuf_tensor(name, list(shape), dt))
        xt = [sbp(f"xt{i}", [128, D]) for i in range(3)]
        xs = [sbp(f"xs{i}", [128, D], BF16) for i in range(2)]
        junk = sbp("junk", [128, D])
        ss = sbp("ss", [128, T])
        rstd = sbp("rstd", [128, T])
        lnv = sbp("lnv", [128, T])
        for t in range(T):
            xb = xt[t % 3]
            P.dma("sp", xb[:], x[t * 128:(t + 1) * 128, :], w=[("xt", t % 3)])
            P.op("act", lambda e, xb=xb, t=t: e.activation(out=junk[:], in_=xb[:], func=AF.Square,
                                                            accum_out=ss[:, t:t + 1]),
                 r=[("xt", t % 3)], w=[("ss", t), "junk_p1"])
            P.op("act", lambda e, t=t: e.activation(out=lnv[:, t:t + 1], in_=ss[:, t:t + 1], func=AF.Ln,
                                                    bias=epsc[:], scale=1.0 / D),
                 r=[("ss", t), "epsc"], w=[("lnv", t)])
            P.op("act", lambda e, t=t: e.activation(out=rstd[:, t:t + 1], in_=lnv[:, t:t + 1], func=AF.Exp,
                                                    scale=-0.5),
                 r=[("lnv", t)], w=[("rstd", t)])
            xsb = xs[t % 2]
            P.op("dve", lambda e, xb=xb, xsb=xsb, t=t: e.tensor_scalar(out=xsb[:], in0=xb[:],
                                                                       scalar1=rstd[:, t:t + 1], scalar2=None,
                                                                       op0=ALU.mult),
                 r=[("xt", t % 3), ("rstd", t)], w=[("xs", t % 2)])
            bk = t % 2
            pT = banks[bk][:, 0:512].bitcast(BF16)
            for c in range(8):
                P.op("pe", lambda e, pT=pT, xsb=xsb, c=c: e.transpose(out=pT[:, c * 128:(c + 1) * 128],
                                                                      in_=xsb[:, c * 128:(c + 1) * 128],
                                                                      identity=ident_b[:]),
                     r=[("xs", t % 2), "ident_b"], w=[("ps", bk)])
            P.op("dve", lambda e, pT=pT, t=t: e.tensor_tensor(
                out=uT[:, :, t * 128:(t + 1) * 128],
                in0=pT.rearrange("p (c j) -> p c j", c=8),
                in1=bcast_ap(gainT[:], [[1, 8], [0, 128]]),
                op=ALU.mult),
                r=[("ps", bk), "gainT"], w=[("uT", t)])
        if "uT" in dbg:
            P.dma("sp", dbg["uT"], uT[:], r=[("uT", t) for t in range(T)])
        for g in range(NG):
            P.dma("sp", uT_scr[:, :, g * 512:(g + 1) * 512], uT[:, :, g * 512:(g + 1) * 512],
                  r=[("uT", t) for t in range(4 * g, 4 * g + 4)], w=[("uT_scr", g)])
        P.emit()

    uT_keys = lambda g: [("uT", t) for t in range(4 * g, 4 * g + 4)]

    if "p2" in phases:
      with ExitStack() as ph:
        def sbp(name, shape, dt=F32):
            return ph.enter_context(nc.sbuf_tensor(name, list(shape), dt))
        cosT = sbp("cosT", [128, S])
        sinT = sbp("sinT", [128, S])
        caus = sbp("caus_sb", [128, 2, 256], BF16)
        sel = sbp("sel_sb", [128, 16, 128], BF16)
        inval = sbp("inval_sb", [128, 16, 16])
        P.dma("sp", cosT[:], ropecos, w=["cosT"])
        P.dma("sp", sinT[:], ropesin, w=["sinT"])
        P.dma("sp", caus[:], caus_in, w=["caus"])
        P.dma("sp", sel[:], sel_in, w=["sel"])
        P.dma("sp", inval[:], inval_in, w=["inval"])
        wnames = ("wq", "wqs", "wk", "wks", "wv")
        wset = [{n: sbp(f"{n}{i}", [128, 8, 128], BF16) for n in wnames} for i in range(2)]
        qT = sbp("qT", [128, S], BF16)
        kT = sbp("kT", [128, S], BF16)
        v_sb = sbp("v_sb", [128, T, 129], BF16)
        maskT = sbp("maskT", [128, S], BF16)
        maskp = sbp("maskp", [128, 4, 128], BF16)
        t1 = [sbp(f"t1_{i}", [128, 512]) for i in range(2)]
        t2 = [sbp(f"t2_{i}", [128, 512]) for i in range(2)]
        r32 = [sbp(f"r32_{i}", [128, 512]) for i in range(2)]
        kmean = sbp("kmean", [128, 16])
        gsb = sbp("gsb", [128, 4, 16])
        m8 = sbp("m8", [128, 4, 8])
        PT = [sbp(f"PT{i}", [128, 512], BF16) for i in range(2)]
        rden = sbp("rden", [128, 2])
        yat = [sbp(f"yat{i}", [128, 128], BF16) for i in range(2)]
        yaT_st = [sbp(f"yaTst{i}", [128, S], BF16) for i in range(2)]
        P.op("pool", lambda e: e.memset(v_sb[:, :, 128:129], 1.0), w=["v_ones"])
        P.op("pool", lambda e: e.memset(maskp[:], 0.0), w=["maskp"])
        P.op("pool", lambda e: e.memset(kmean[:], 0.0), w=["kmean"])
        scale = float(HD) ** -0.5

        def load_w(h):
            ws = wset[h % 2]
            i = h % 2
            cq, ck, cv = h * 128, 1024 + h * 128, 2048 + h * 128
            P.dma("pool", ws["wq"][:], w_in_v[:, :, cq:cq + 128], w=[("w", i, "wq")])
            P.dma("pool", ws["wqs"][:, :, 0:64], w_in_v[:, :, cq + 64:cq + 128], w=[("w", i, "wqs0")])
            P.dma("pool", ws["wqs"][:, :, 64:128], w_in_v[:, :, cq:cq + 64], w=[("w", i, "wqs1")])
            P.dma("pool", ws["wk"][:], w_in_v[:, :, ck:ck + 128], w=[("w", i, "wk")])
            P.dma("pool", ws["wks"][:, :, 0:64], w_in_v[:, :, ck + 64:ck + 128], w=[("w", i, "wks0")])
            P.dma("pool", ws["wks"][:, :, 64:128], w_in_v[:, :, ck:ck + 64], w=[("w", i, "wks1")])
            P.dma("pool", ws["wv"][:], w_in_v[:, :, cv:cv + 128], w=[("w", i, "wv")])

        def wkeys(i, n):
            if n in ("wqs", "wks"):
                return [("w", i, n + "0"), ("w", i, n + "1")]
            return [("w", i, n)]

        def rope_proj(h, g, wa, wb, dstT, is_k):
            i = h % 2
            ws = wset[i]
            ba, bb = (0, 1) if g % 2 == 0 else (2, 3)
            for (bk, wn) in ((ba, wa), (bb, wb)):
                for kc in range(8):
                    P.op("pe", lambda e, bk=bk, wn=wn, kc=kc: e.matmul(
                        banks[bk][:, :], lhsT=ws[wn][:, kc, :], rhs=uT[:, kc, g * 512:(g + 1) * 512],
                        start=(kc == 0), stop=(kc == 7)),
                        r=wkeys(i, wn) + uT_keys(g), w=[("ps", bk)])
            j = g % 2
            P.op("dve", lambda e: e.tensor_tensor(out=t1[j][:], in0=banks[ba][:, :], in1=cosT[:, g * 512:(g + 1) * 512],
                                                  op=ALU.mult),
                 r=[("ps", ba), "cosT"], w=[("t1", j)])
            P.op("dve", lambda e: e.tensor_tensor(out=t2[j][:], in0=banks[bb][:, :], in1=sinT[:, g * 512:(g + 1) * 512],
                                                  op=ALU.mult),
                 r=[("ps", bb), "sinT"], w=[("t2", j)])
            P.op("pool", lambda e: e.tensor_tensor(out=r32[j][:], in0=t1[j][:], in1=t2[j][:], op=ALU.add),
                 r=[("t1", j), ("t2", j)], w=[("r32", j)])
            P.op("act", lambda e: e.activation(out=dstT[:, g * 512:(g + 1) * 512], in_=r32[j][:], func=AF.Copy),
                 r=[("r32", j)], w=[("qk", is_k, g)])
            return j

        def do_head(hi, h):
            if hi == 0:
                load_w(h)
            i = h % 2
            ws = wset[i]
            for g in range(NG):
                j = rope_proj(h, g, "wk", "wks", kT, 1)
                P.op("dve", lambda e, j=j, g=g: e.tensor_reduce(
                    out=kmean[:, 2 * g:2 * g + 2], in_=r32[j][:].rearrange("p (b q) -> p b q", q=256),
                    axis=AX.X, op=ALU.add),
                    r=[("r32", j)], w=["kmean"])
            for t in range(T):
                for kc in range(8):
                    P.op("pe", lambda e, t=t, kc=kc: e.matmul(
                        banks[6][:, (t % 4) * 128:(t % 4 + 1) * 128], lhsT=uT[:, kc, t * 128:(t + 1) * 128],
                        rhs=ws["wv"][:, kc, :], start=(kc == 0), stop=(kc == 7)),
                        r=wkeys(i, "wv") + [("uT", t)], w=[("ps", 6)])
                if t % 4 == 3:
                    P.op("act", lambda e, t=t: e.activation(
                        out=v_sb[:, t - 3:t + 1, 0:128], in_=banks[6][:, :].rearrange("p (a d) -> p a d", d=128),
                        func=AF.Copy),
                        r=[("ps", 6)], w=[("v", t // 4)])
            for g in range(NG):
                j = rope_proj(h, g, "wq", "wqs", qT, 0)
                for tt in range(4):
                    P.op("pe", lambda e, j=j, tt=tt: e.matmul(
                        banks[4][:, tt * 16:(tt + 1) * 16], lhsT=r32[j][:, tt * 128:(tt + 1) * 128],
                        rhs=kmean[:, :], start=True, stop=True),
                        r=[("r32", j), "kmean"], w=[("ps", 4)])
                P.op("dve", lambda e, g=g: e.tensor_tensor(
                    out=gsb[:].rearrange("p (b t) c -> p b t c", b=2),
                    in0=banks[4][:, 0:64].rearrange("p (b t c) -> p b t c", b=2, t=2),
                    in1=bcast_ap(inval[:, 2 * g, :], [[16, 2], [0, 2], [1, 16]]),
                    op=ALU.add),
                    r=[("ps", 4), "inval"], w=["gsb"])
                for tt in range(4):
                    P.op("dve", lambda e, tt=tt: e.max(out=m8[:, tt, :], in_=gsb[:, tt, :]),
                         r=["gsb"], w=[("m8", tt)])
                P.op("dve", lambda e: e.tensor_tensor(
                    out=maskp[:, :, 0:16], in0=gsb[:],
                    in1=bcast_ap(m8[:, 0, 2:3], [[8, 4], [0, 16]]), op=ALU.is_lt),
                    r=["gsb"] + [("m8", tt) for tt in range(4)], w=["maskp"])
                pM = banks[5][:, 0:256].bitcast(BF16)
                for tt in range(4):
                    P.op("pe", lambda e, tt=tt, pM=pM: e.transpose(
                        out=pM[:, tt * 128:(tt + 1) * 128], in_=maskp[:, tt, :], identity=ident_b[:]),
                        r=["maskp", "ident_b"], w=[("ps", 5)])
                P.op("act", lambda e, g=g, pM=pM: e.activation(
                    out=maskT[:, g * 512:(g + 1) * 512], in_=pM, func=AF.Copy, scale=NEG),
                    r=[("ps", 5)], w=[("maskT", g)])
            if hi + 1 < len(heads):
                load_w(heads[hi + 1])
            yst = yaT_st[hi % 2]
            steps = [(qb, kb) for qb in range(NB) for kb in range(qb + 1)]

            def emit_scores(i):
                qb, kb = steps[i]
                g = qb // 2
                bs = i % 2
                for kc in range(2):
                    kt = kb * 2 + kc
                    P.op("pe", lambda e, bs=bs, kc=kc, kt=kt, qb=qb: e.matmul(
                        banks[bs][:, kc * 256:(kc + 1) * 256], lhsT=kT[:, kt * 128:(kt + 1) * 128],
                        rhs=qT[:, qb * 256:(qb + 1) * 256], start=True, stop=False),
                        r=[("qk", 1, kt // 4), ("qk", 0, g)], w=[("ps", bs)])
                    if kb < qb:
                        P.op("pe", lambda e, bs=bs, kc=kc, kb=kb, qb=qb: e.matmul(
                            banks[bs][:, kc * 256:(kc + 1) * 256], lhsT=sel[:, kb, :],
                            rhs=maskT[:, qb * 256:(qb + 1) * 256], start=False, stop=True),
                            r=["sel", ("maskT", g)], w=[("ps", bs)])
                    else:
                        P.op("pe", lambda e, bs=bs, kc=kc: e.matmul(
                            banks[bs][:, kc * 256:(kc + 1) * 256], lhsT=ident_b[:],
                            rhs=caus[:, kc, :], start=False, stop=True),
                            r=["ident_b", "caus"], w=[("ps", bs)])
                P.op("act", lambda e, bs=bs: e.activation(out=PT[bs][:], in_=banks[bs][:, :], func=AF.Exp, scale=scale),
                     r=[("ps", bs)], w=[("PT", bs)])

            def emit_pv(i):
                qb, kb = steps[i]
                bs = i % 2
                for kc in range(2):
                    kt = kb * 2 + kc
                    for hf in range(2):
                        P.op("pe", lambda e, bs=bs, kc=kc, kt=kt, hf=hf, kb=kb, qb=qb: e.matmul(
                            banks[2 + hf][:, 0:129], lhsT=PT[bs][:, kc * 256 + hf * 128:kc * 256 + hf * 128 + 128],
                            rhs=v_sb[:, kt, :], start=(kb == 0 and kc == 0), stop=(kb == qb and kc == 1)),
                            r=[("PT", bs), ("v", kt // 4), "v_ones"], w=[("ps", 2 + hf)])
                if kb == qb:
                    pY = banks[5][:, 0:128].bitcast(BF16)
                    for hf in range(2):
                        P.op("dve", lambda e, hf=hf: e.reciprocal(out=rden[:, hf:hf + 1], in_=banks[2 + hf][:, 128:129]),
                             r=[("ps", 2 + hf)], w=[("rden", hf)])
                        P.op("dve", lambda e, hf=hf: e.tensor_scalar(out=yat[hf][:], in0=banks[2 + hf][:, 0:128],
                                                                   scalar1=rden[:, hf:hf + 1], scalar2=None, op0=ALU.mult),
                             r=[("ps", 2 + hf), ("rden", hf)], w=[("yat", hf)])
                    pend_fin.append((qb, pY))

            def emit_fin():
                while pend_fin:
                    qb, pY = pend_fin.pop(0)
                    for hf in range(2):
                        P.op("pe", lambda e, hf=hf, pY=pY: e.transpose(out=pY[:, hf * 128:(hf + 1) * 128], in_=yat[hf][:],
                                                                     identity=ident_b[:]),
                             r=[("yat", hf), "ident_b"], w=[("ps", 5)])
                    P.op("dve", lambda e, qb=qb, pY=pY, yst=yst: e.tensor_copy(out=yst[:, qb * 256:(qb + 1) * 256], in_=pY),
                         r=[("ps", 5)], w=[("yst", hi % 2, qb)])

            pend_fin = []
            emit_scores(0)
            for i in range(len(steps)):
                if i + 1 < len(steps):
                    emit_scores(i + 1)
                emit_fin()
                emit_pv(i)
            emit_fin()
            P.dma("sp", yaT_scr[h], yst[:], r=[("yst", hi % 2, qb) for qb in range(NB)], w=[("yaT_scr", h)])

        for hi, h in enumerate(heads):
            do_head(hi, h)
        P.emit()

    es12.close()

    def mm(out_, lhsT, rhs, r, w, start=True, stop=True):
        P.op("pe", lambda e: e.matmul(out_, lhsT=lhsT, rhs=rhs, start=start, stop=stop), r=r, w=w)

    def tr(out_, in_, r, w):
        P.op("pe", lambda e: e.transpose(out=out_, in_=in_, identity=ident_b[:]), r=list(r) + ["ident_b"], w=w)

    evc = [0]

    def ev(out_, in_, r, w, eng=None):
        if eng is None:
            eng = ("act", "dve", "act")[evc[0] % 3]
            evc[0] += 1
        if eng == "act":
            P.op("act", lambda e: e.activation(out=out_, in_=in_, func=AF.Copy), r=r, w=w)
        else:
            P.op(eng, lambda e: e.tensor_copy(out=out_, in_=in_), r=r, w=w)

    def tt_op(eng, out_, in0, in1, op, r, w):
        P.op(eng, lambda e: e.tensor_tensor(out=out_, in0=in0, in1=in1, op=op), r=r, w=w)

    def pbc(ap_, n):
        return bass.AP(ap_.tensor, ap_.offset, [[0, 128], [1, n]])

    if "p3" in phases:
      with ExitStack() as ph:
        def sbp(name, shape, dt=F32):
            return ph.enter_context(nc.sbuf_tensor(name, list(shape), dt))
        M1 = sbp("gM1s", [128, 128]); M2 = sbp("gM2s", [128, 128]); Mc = sbp("gMcs", [128, 2, 128])
        causneg = sbp("gcausnegs", [128, 128], BF16); strict = sbp("gstricts", [128, 128], BF16)
        rowmask = sbp("growmasks", [128, 2])
        ones_f = sbp("ones_f", [128, 128]); ones_b = sbp("ones_b", [128, 128], BF16)
        onec = sbp("onec", [128, 1]); eps128 = sbp("eps128", [128, 1])
        cw = sbp("cw", [128, 4, 24]); dtb = sbp("dtb", [128, 8]); negA = sbp("negA", [128, 8])
        ggain = sbp("ggain", [128, 128])
        wqkv = sbp("wqkv", [128, 8, 3072], BF16); wba = sbp("wba", [128, 8, 16], BF16)
        halo = sbp("halo", [128, 24, 3], BF16)
        uTg = [sbp("uTg0", [128, 8, 512], BF16)] * 2
        pre = [sbp(f"pre{i}", [128, 515], BF16) for i in range(2)]
        dg = [sbp(f"dg{i}", [128, 4, 128], BF16) for i in range(2)]
        s_all = sbp("s_all", [128, 24, 512], BF16)
        sq = [sbp(f"sq{i}", [128, 512], BF16) for i in range(2)]
        rr = [sbp(f"rr{i}", [128, 512]) for i in range(2)]
        k_tok = sbp("k_tok", [128, 4, 8, 128], BF16); v_tok = sbp("v_tok", [128, 4, 8, 128], BF16)
        eb = sbp("eb", [128, 4, 8]); beta = sbp("beta", [128, 4, 8]); xa = sbp("xa", [128, 4, 8])
        gg = sbp("gg", [128, 4, 16])
        ex = [sbp(f"ex{i}", [128, 64]) for i in range(2)]; cb = sbp("cb", [128, 8]); edm = sbp("edm", [128, 2, 8])
        G2 = sbp("G2", [128, 8, 128])
        Ec = sbp("Ec", [128, 8, 128], BF16); Es = sbp("Es", [128, 8, 128], BF16)
        attn = sbp("attn", [128, 8, 128], BF16); attnT = [sbp(f"attnT{i}", [128, 8, 128], BF16) for i in range(2)]
        Pp = [sbp(f"Pp{i}", [128, 8, 128], BF16) for i in range(2)]
        Qq = [sbp(f"Qq{i}", [128, 8, 128], BF16) for i in range(2)]
        Yy = [sbp(f"Yy{i}", [128, 8, 128], BF16) for i in range(2)]
        egrep = sbp("egrep", [128, 8, 128])
        qgT = [sbp(f"qgT{i}", [128, 8, 128], BF16) for i in range(2)]
        kbg = sbp("kbg", [128, 8, 128], BF16); vb = sbp("vb", [128, 8, 128], BF16)
        kd = [sbp(f"kd{i}", [128, 2, 8, 128], BF16) for i in range(2)]
        wT = [sbp(f"wT{i}", [128, 8, 128], BF16) for i in range(2)]; u32 = [sbp(f"u32_{i}", [128, 8, 128]) for i in range(2)]
        osq_t = sbp("osq_t", [128, 8, 128])
        v_new = sbp("v_new", [128, 8, 128], BF16)
        S32 = sbp("S32", [128, 8, 128]); S_bf = sbp("S_bf", [128, 8, 128], BF16)
        o32 = sbp("o32", [128, 8, 128]); oss = sbp("oss", [128, 8]); orstd = sbp("orstd", [128, 8])
        ynb = sbp("ynb", [128, 8, 128], BF16)
        ynT_st = sbp("ynT_st", [128, 8, 512], BF16)

        for (tile_, src, k) in ((M1, gM1_in, "M1"), (M2, gM2_in, "M2"), (Mc, gMc_in, "Mc"), (causneg, gcausneg_in, "causneg"),
                                (strict, gstrict_in, "strict"), (rowmask, growmask_in, "rowmask")):
            P.dma("sp", tile_[:], src, w=[k])
        for i in range(4):
            P.dma("sp", cw[:, i, :], conv_w[i].rearrange("(c p) -> p c", p=128), w=["cw"])
        P.dma("sp", dtb[:], pbc(dt_bias, 8), w=["dtb"])
        P.dma("sp", negA[:], pbc(a_log, 8), w=["negA0"])
        P.dma("sp", ggain[:], pbc(gdn_norm, 128), w=["ggain"])
        for j in range(6):
            P.dma("pool", wqkv[:, :, j * 512:(j + 1) * 512], w_in_v[:, :, 3072 + j * 512:3072 + (j + 1) * 512],
                  w=[("wqkv", j)])
        P.dma("pool", wba[:], w_in_v[:, :, 7168:7184], w=["wba"])
        P.op("pool", lambda e: e.memset(ones_f[:], 1.0), w=["ones_f"])
        P.op("pool", lambda e: e.memset(ones_b[:], 1.0), w=["ones_b"])
        P.op("pool", lambda e: e.memset(onec[:], 1.0), w=["onec"])
        P.op("pool", lambda e: e.memset(eps128[:], 128.0 * EPS), w=["eps128"])
        P.op("pool", lambda e: e.memset(halo[:], 0.0), w=["halo"])
        P.op("pool", lambda e: e.memset(gg[:], 0.0), w=["gg"])
        P.op("pool", lambda e: e.memset(v_new[:], 0.0), w=[("v_new", 0), ("v_new", 1)])
        P.op("pool", lambda e: e.memset(S32[:], 0.0), w=[("S32", 0), ("S32", 1)])
        P.op("pool", lambda e: e.memset(S_bf[:], 0.0), w=[("S_bf", 0), ("S_bf", 1)])
        P.op("act", lambda e: e.activation(out=negA[:], in_=negA[:], func=AF.Exp), r=["negA0"], w=["negA1"])
        P.op("dve", lambda e: e.tensor_scalar(out=negA[:], in0=negA[:], scalar1=-1.0, scalar2=None, op0=ALU.mult),
             r=["negA1"], w=["negA"])

        rot = [0]

        def nb():
            b = 4 + rot[0] % 4
            rot[0] += 1
            return b

        def hview(bk):
            return banks[bk][:, :].rearrange("p (h d) -> p h d", d=128)

        def group_stage(g):
            ug = uTg[g % 2]
            P.dma("sp", ug[:], uT_scr[:, :, g * 512:(g + 1) * 512], r=[("uT_scr", g)], w=[("uTg", 0)])
            def proj_cc(cc):
                i2 = cc % 2
                bk = nb()
                for kc in range(8):
                    mm(banks[bk][:, :], wqkv[:, kc, cc * 128:(cc + 1) * 128], ug[:, kc, :],
                       r=[("wqkv", cc // 4), ("uTg", 0)], w=[("ps", bk)], start=(kc == 0), stop=(kc == 7))
                ev(pre[i2][:, 3:515], banks[bk][:, :], r=[("ps", bk)], w=[("pre", i2)], eng="dve")
                P.op("pool", lambda e, i2=i2, cc=cc: e.tensor_copy(out=pre[i2][:, 0:3], in_=halo[:, cc, :]),
                     r=["halo", ("pre", i2)], w=[("pre", i2)])
                P.op("pool", lambda e, i2=i2, cc=cc: e.tensor_copy(out=halo[:, cc, :], in_=pre[i2][:, 512:515]),
                     r=[("pre", i2)], w=["halo"])
                tt_op("pool", dg[i2][:], bcast_ap(ident_f[:], [[0, 4], [1, 128]]),
                      bcast_ap(cw[:, 0, cc:cc + 1], [[24, 4], [0, 128]]), ALU.mult, r=["ident_f", "cw"], w=[("dg", i2)])

            def conv_cc(cc):
                i2 = cc % 2
                bk2 = nb()
                for i in range(4):
                    mm(banks[bk2][:, :], dg[i2][:, i, :], pre[i2][:, i:i + 512],
                       r=[("dg", i2), ("pre", i2)], w=[("ps", bk2)], start=(i == 0), stop=(i == 3))
                P.op("act", lambda e, cc=cc, bk2=bk2: e.activation(out=s_all[:, cc, :], in_=banks[bk2][:, :], func=AF.Silu),
                     r=[("ps", bk2)], w=[("s", cc)])

            proj_cc(0)
            for cc in range(24):
                if cc + 1 < 24:
                    proj_cc(cc + 1)
                conv_cc(cc)
            for cc in range(16):
                i2 = cc % 2
                tt_op("pool", sq[i2][:], s_all[:, cc, :], s_all[:, cc, :], ALU.mult, r=[("s", cc)], w=[("sq", i2)])
                bk = nb()
                mm(banks[bk][:, :], ones_b[:], sq[i2][:], r=["ones_b", ("sq", i2)], w=[("ps", bk)])
                if cc < 8:
                    P.op("act", lambda e, i2=i2, bk=bk: e.activation(out=rr[i2][:], in_=banks[bk][:, :], func=AF.Ln,
                                                                      bias=eps128[:], scale=128.0),
                         r=[("ps", bk), "eps128"], w=[("rr", i2)])
                else:
                    P.op("act", lambda e, i2=i2, bk=bk: e.activation(out=rr[i2][:], in_=banks[bk][:, :], func=AF.Ln,
                                                                      bias=epsc[:], scale=1.0),
                         r=[("ps", bk), "epsc"], w=[("rr", i2)])
                P.op("act", lambda e, i2=i2: e.activation(out=rr[i2][:], in_=rr[i2][:], func=AF.Exp, scale=-0.5),
                     r=[("rr", i2)], w=[("rr", i2)])
                tt_op("dve", s_all[:, cc, :], s_all[:, cc, :], rr[i2][:], ALU.mult, r=[("s", cc), ("rr", i2)], w=[("s", cc)])
            for tt in range(4):
                for (base, dst, nm) in ((8, k_tok, "k_tok"), (16, v_tok, "v_tok")):
                    bk = nb()
                    pv = banks[bk][:, :].bitcast(BF16)
                    for h in range(8):
                        tr(pv[:, h * 128:(h + 1) * 128], s_all[:, base + h, tt * 128:(tt + 1) * 128],
                           r=[("s", base + h)], w=[("ps", bk)])
                    ev(dst[:, tt, :, :], pv.rearrange("p (h d) -> p h d", d=128), r=[("ps", bk)], w=[(nm, tt)])
            bk = nb()
            for tt in range(4):
                for kc in range(8):
                    mm(banks[bk][:, tt * 16:(tt + 1) * 16], ug[:, kc, tt * 128:(tt + 1) * 128], wba[:, kc, :],
                       r=[("uTg", 0), "wba"], w=[("ps", bk)], start=(kc == 0), stop=(kc == 7))
            bav = banks[bk][:, 0:64].rearrange("p (t c) -> p t c", c=16)
            P.op("act", lambda e: e.activation(out=eb[:], in_=bav[:, :, 0:8], func=AF.Exp, scale=-1.0),
                 r=[("ps", bk)], w=["eb"])
            P.op("dve", lambda e: e.tensor_scalar(out=eb[:], in0=eb[:], scalar1=1.0, scalar2=None, op0=ALU.add),
                 r=["eb"], w=["eb"])
            P.op("dve", lambda e: e.reciprocal(out=beta[:], in_=eb[:]), r=["eb"], w=["beta"])
            tt_op("dve", xa[:], bav[:, :, 8:16], bcast_ap(dtb[:], [[0, 4], [1, 8]]), ALU.add, r=[("ps", bk), "dtb"], w=["xa"])
            P.op("act", lambda e: e.activation(out=xa[:], in_=xa[:], func=AF.Exp), r=["xa"], w=["xa"])
            P.op("act", lambda e: e.activation(out=xa[:], in_=xa[:], func=AF.Ln, bias=onec[:], scale=1.0),
                 r=["xa", "onec"], w=["xa"])
            tt_op("dve", gg[:, :, 0:8], xa[:], bcast_ap(negA[:], [[0, 4], [1, 8]]), ALU.mult, r=["xa", "negA"], w=["gg"])

        def pair_prep(g, tt, par):
            tsl = slice(tt * 128, (tt + 1) * 128)
            gv = gg[:, tt, 0:8]
            gv16 = gg[:, tt, :]
            bv = beta[:, tt, :]
            bk = nb()
            mm(banks[bk][:, 0:16], M1[:], gv16, r=["M1", "gg"], w=[("ps", bk)])
            mm(banks[bk][:, 16:32], M2[:], gv16, r=["M2", "gg"], w=[("ps", bk)])
            mm(banks[bk][:, 32:48], Mc[:, 0, :], gv16, r=["Mc", "gg"], w=[("ps", bk)])
            mm(banks[bk][:, 48:64], Mc[:, 1, :], gv16, r=["Mc", "gg"], w=[("ps", bk)])
            P.op("act", lambda e, bk=bk: e.activation(out=ex[par][:], in_=banks[bk][:, 0:64], func=AF.Exp), r=[("ps", bk)], w=[("ex", par)])
            tt_op("dve", cb[:], bv, ex[par][:, 0:8], ALU.mult, r=["beta", ("ex", par)], w=["cb"])
            for c in range(2):
                P.op("dve", lambda e, c=c: e.tensor_scalar(out=edm[:, c, :], in0=ex[par][:, 16:24], scalar1=rowmask[:, c:c + 1],
                                                         scalar2=None, op0=ALU.mult),
                     r=[("ex", par), "rowmask"], w=[("edm", c)])
            yield
            tt_op("pool", G2[:], bcast_ap(M1[:], [[0, 8], [1, 128]]), bcast_ap(gv, [[1, 8], [0, 128]]), ALU.mult,
                  r=["M1", "gg"], w=["G2"])
            for hg in range(2):
                bk = nb()
                for hh in range(4):
                    h = hg * 4 + hh
                    mm(banks[bk][:, hh * 128:(hh + 1) * 128], G2[:, h, :], M2[:], r=["G2", "M2"], w=[("ps", bk)],
                       start=True, stop=False)
                    mm(banks[bk][:, hh * 128:(hh + 1) * 128], ident_b[:], causneg[:], r=["ident_b", "causneg"],
                       w=[("ps", bk)], start=False, stop=True)
                P.op("act", lambda e, hg=hg, bk=bk: e.activation(out=Ec[:, hg * 4:(hg + 1) * 4, :], in_=hview(bk), func=AF.Exp),
                     r=[("ps", bk)], w=[("Ec", hg)])
            tt_op("pool", Es[:], Ec[:], bcast_ap(strict[:], [[0, 8], [1, 128]]), ALU.mult,
                  r=[("Ec", 0), ("Ec", 1), "strict"], w=["Es"])
            tt_op("pool", Es[:], Es[:], bcast_ap(bv, [[1, 8], [0, 128]]), ALU.mult, r=["Es", "beta"], w=["Es"])
            for hg in range(2):
                bk = nb()
                for hh in range(4):
                    h = hg * 4 + hh
                    mm(banks[bk][:, hh * 128:(hh + 1) * 128], ones_f[:], G2[:, h, :], r=["G2", "ones_f"], w=[("ps", bk)])
                P.op("act", lambda e, hg=hg, bk=bk: e.activation(out=egrep[:, hg * 4:(hg + 1) * 4, :], in_=hview(bk), func=AF.Exp),
                     r=[("ps", bk)], w=[("egrep", hg)])
            tt_op("pool", qgT[par][:], s_all[:, 0:8, tsl], egrep[:], ALU.mult,
                  r=[("s", c_) for c_ in range(8)] + [("egrep", 0), ("egrep", 1)], w=[("qgT", par)])
            yield
            A_ = Qq[0]
            B_ = Pp[0]
            for hg in range(2):
                bk = nb()
                for hh in range(4):
                    h = hg * 4 + hh
                    mm(banks[bk][:, hh * 128:(hh + 1) * 128], s_all[:, 8 + h, tsl], s_all[:, 8 + h, tsl],
                       r=[("s", 8 + h)], w=[("ps", bk)])
                tt_op("dve", A_[:, hg * 4:(hg + 1) * 4, :], hview(bk), Es[:, hg * 4:(hg + 1) * 4, :], ALU.mult,
                      r=[("ps", bk), "Es"], w=[("Q", 0, hg)])
                yield
            for hg in range(2):
                bk = nb()
                for hh in range(4):
                    h = hg * 4 + hh
                    mm(banks[bk][:, hh * 128:(hh + 1) * 128], s_all[:, h, tsl], s_all[:, 8 + h, tsl],
                       r=[("s", h), ("s", 8 + h)], w=[("ps", bk)])
                tt_op("dve", attn[:, hg * 4:(hg + 1) * 4, :], hview(bk), Ec[:, hg * 4:(hg + 1) * 4, :], ALU.mult,
                      r=[("ps", bk), ("Ec", hg)], w=[("attn", hg)])
            for (src, dst, rk, wk) in ((A_, B_, [("Q", 0, 0), ("Q", 0, 1)], [("P", 0, 0), ("P", 0, 1)]),
                                       (attn, attnT[par], [("attn", 0), ("attn", 1)], [("attnT", par)])):
                bk = nb()
                pv = banks[bk][:, :].bitcast(BF16)
                for h in range(8):
                    tr(pv[:, h * 128:(h + 1) * 128], src[:, h, :], r=rk, w=[("ps", bk)])
                ev(dst[:], pv.rearrange("p (h d) -> p h d", d=128), r=[("ps", bk)], w=wk)
            yield
            tt_op("pool", Yy[0][:], bcast_ap(ident_b[:], [[0, 8], [1, 128]]), B_[:], ALU.subtract,
                  r=["ident_b", ("P", 0, 0), ("P", 0, 1)], w=[("Y", 0, 0), ("Y", 0, 1)])
            cur = 0
            for lvl in range(1, 6):
                nxt = 1 - cur
                for hg in range(2):
                    bk = nb()
                    for hh in range(4):
                        h = hg * 4 + hh
                        mm(banks[bk][:, hh * 128:(hh + 1) * 128], Pp[cur][:, h, :], Qq[cur][:, h, :],
                           r=[("P", cur, hg), ("Q", cur, hg)], w=[("ps", bk)])
                    ev(Qq[nxt][:, hg * 4:(hg + 1) * 4, :], hview(bk), r=[("ps", bk)], w=[("Q", nxt, hg)])
                    if lvl < 5:
                        bk = nb()
                        for hh in range(4):
                            h = hg * 4 + hh
                            mm(banks[bk][:, hh * 128:(hh + 1) * 128], Qq[cur][:, h, :], Pp[cur][:, h, :],
                               r=[("P", cur, hg), ("Q", cur, hg)], w=[("ps", bk)])
                        ev(Pp[nxt][:, hg * 4:(hg + 1) * 4, :], hview(bk), r=[("ps", bk)], w=[("P", nxt, hg)])
                    bk = nb()
                    for hh in range(4):
                        h = hg * 4 + hh
                        mm(banks[bk][:, hh * 128:(hh + 1) * 128], ident_b[:], Yy[cur][:, h, :],
                           r=["ident_b", ("Y", cur, hg)], w=[("ps", bk)], start=True, stop=False)
                        mm(banks[bk][:, hh * 128:(hh + 1) * 128], Qq[nxt][:, h, :], Yy[cur][:, h, :],
                           r=[("Q", nxt, hg), ("Y", cur, hg)], w=[("ps", bk)], start=False, stop=True)
                    ev(Yy[nxt][:, hg * 4:(hg + 1) * 4, :], hview(bk), r=[("ps", bk)], w=[("Y", nxt, hg)])
                cur = nxt
                yield
            XT = Yy[cur]
            xk = lambda hg: ("Y", cur, hg)
            yield
            tt_op("dve", kbg[:], k_tok[:, tt, :, :], bcast_ap(cb[:], [[1, 8], [0, 128]]), ALU.mult,
                  r=[("k_tok", tt), "cb"], w=["kbg"])
            tt_op("dve", vb[:], v_tok[:, tt, :, :], bcast_ap(bv, [[1, 8], [0, 128]]), ALU.mult,
                  r=[("v_tok", tt), "beta"], w=["vb"])
            for c in range(2):
                tt_op("pool", kd[par][:, c, :, :], k_tok[:, tt, :, :], bcast_ap(edm[:, c, :], [[1, 8], [0, 128]]), ALU.mult,
                      r=[("k_tok", tt), ("edm", c)], w=[("kd", par, c)])
            for hg in range(2):
                bk = nb()
                for hh in range(4):
                    h = hg * 4 + hh
                    mm(banks[bk][:, hh * 128:(hh + 1) * 128], kbg[:, h, :], XT[:, h, :], r=["kbg", xk(hg)], w=[("ps", bk)])
                ev(wT[par][:, hg * 4:(hg + 1) * 4, :], hview(bk), r=[("ps", bk)], w=[("wT", par, hg)])
                bk = nb()
                for hh in range(4):
                    h = hg * 4 + hh
                    mm(banks[bk][:, hh * 128:(hh + 1) * 128], XT[:, h, :], vb[:, h, :], r=["vb", xk(hg)], w=[("ps", bk)])
                ev(u32[par][:, hg * 4:(hg + 1) * 4, :], hview(bk), r=[("ps", bk)], w=[("u32", par, hg)])
                yield
        def pair_rec(g, tt, par):
            tsl = slice(tt * 128, (tt + 1) * 128)
            for c in range(2):
                rs = slice(c * 64, (c + 1) * 64)
                for hg in range(2):
                    hs = slice(hg * 4, (hg + 1) * 4)
                    bw = hg
                    for hh in range(4):
                        h = hg * 4 + hh
                        mm(banks[bw][:, hh * 128:(hh + 1) * 128], wT[par][:, h, :], S_bf[:, h, :],
                           r=[("wT", par, hg), ("S_bf", hg)], w=[("ps", bw)])
                    P.op("dve", lambda e, rs=rs, hs=hs, bw=bw: e.tensor_tensor(
                        out=v_new[rs, hs, :], in0=u32[par][rs, hs, :], in1=hview(bw)[rs, :, :], op=ALU.subtract),
                        r=[("u32", par, hg), ("ps", bw)], w=[("v_new", hg)])
                for hg in range(2):
                    hs = slice(hg * 4, (hg + 1) * 4)
                    bo = 2 + hg
                    for hh in range(4):
                        h = hg * 4 + hh
                        mm(banks[bo][:, hh * 128:(hh + 1) * 128], qgT[par][:, h, :], S_bf[:, h, :],
                           r=[("qgT", par), ("S_bf", hg)], w=[("ps", bo)], start=True, stop=False)
                        mm(banks[bo][:, hh * 128:(hh + 1) * 128], attnT[par][:, h, :], v_new[:, h, :],
                           r=[("attnT", par), ("v_new", hg)], w=[("ps", bo)], start=False, stop=True)
                    P.op("act", lambda e, rs=rs, hs=hs, bo=bo: e.activation(out=o32[rs, hs, :], in_=hview(bo)[rs, :, :], func=AF.Copy),
                         r=[("ps", bo)], w=[("o32", hg)])
                    bd = nb()
                    for hh in range(4):
                        h = hg * 4 + hh
                        mm(banks[bd][:, hh * 128:(hh + 1) * 128], kd[par][:, c, h, :], v_new[:, h, :],
                           r=[("kd", par, c), ("v_new", hg)], w=[("ps", bd)])
                    tt_op("pool", S32[:, hs, :], S32[:, hs, :], bcast_ap(ex[par][:, 32 + 16 * c + hg * 4:32 + 16 * c + hg * 4 + 1],
                                                                       [[1, 4], [0, 128]]), ALU.mult,
                          r=[("S32", hg), ("ex", par)], w=[("S32", hg)])
                    tt_op("dve", S32[:, hs, :], S32[:, hs, :], hview(bd), ALU.add, r=[("S32", hg), ("ps", bd)], w=[("S32", hg)])
                    P.op("act", lambda e, hs=hs: e.activation(out=S_bf[:, hs, :], in_=S32[:, hs, :], func=AF.Copy),
                         r=[("S32", hg)], w=[("S_bf", hg)])
                    yield
            osq = osq_t
            tt_op("pool", osq[:], o32[:], o32[:], ALU.mult, r=[("o32", 0), ("o32", 1)], w=["osq"])
            P.op("dve", lambda e: e.tensor_reduce(out=oss[:], in_=osq[:], axis=AX.X, op=ALU.add), r=["osq"], w=["oss"])
            P.op("act", lambda e: e.activation(out=orstd[:], in_=oss[:], func=AF.Ln, bias=epsc[:], scale=1.0 / HD),
                 r=["oss", "epsc"], w=["orstd"])
            P.op("act", lambda e: e.activation(out=orstd[:], in_=orstd[:], func=AF.Exp, scale=-0.5), r=["orstd"], w=["orstd"])
            tt_op("dve", o32[:], o32[:], bcast_ap(orstd[:], [[1, 8], [0, 128]]), ALU.mult,
                  r=[("o32", 0), ("o32", 1), "orstd"], w=[("o32", 0), ("o32", 1)])
            tt_op("pool", ynb[:], o32[:], bcast_ap(ggain[:], [[0, 8], [1, 128]]), ALU.mult,
                  r=[("o32", 0), ("o32", 1), "ggain"], w=["ynb"])
            bk = nb()
            pv = banks[bk][:, :].bitcast(BF16)
            for h in range(8):
                tr(pv[:, h * 128:(h + 1) * 128], ynb[:, h, :], r=["ynb"], w=[("ps", bk)])
            ev(ynT_st[:, :, tsl], pv.rearrange("p (h d) -> p h d", d=128), r=[("ps", bk)], w=[("ynT_st", tt)])
            if "gdn_dbg" in debug and g == 0 and tt == 0:
                for (nm, tl, dt_, keys) in ((("ex", par), ex, F32, [("ex", par)]), ("S32", S32, F32, [("S32", 0), ("S32", 1)]),
                                            ("o32", o32, F32, [("o32", 0), ("o32", 1)]), ("wT", wT, BF16, [("wT", 0), ("wT", 1)]),
                                            ("u32", u32, F32, [("u32", 0), ("u32", 1)]), ("gg", gg, F32, ["gg"]),
                                            ("beta", beta, F32, ["beta"]), ("kd", kd, BF16, [("kd", 0), ("kd", 1)]),
                                            ("k_tok", k_tok, BF16, [("k_tok", 0)]), ("v_new", v_new, BF16, [("v_new", 0), ("v_new", 1)])):
                    dd = nc.dram_tensor("dbg_" + nm, list(tl.shape), dt_, kind="ExternalOutput").ap()
                    P.dma("sp", dd, tl[:], r=keys)

            yield

        def interleave(gp, gr):
            done_p = gp is None
            done_r = gr is None
            while not (done_p and done_r):
                if not done_r:
                    try:
                        next(gr)
                    except StopIteration:
                        done_r = True
                if not done_p:
                    try:
                        next(gp)
                    except StopIteration:
                        done_p = True

        pairs = [(g, tt) for g in range(NG) for tt in range(4)]
        group_stage(0)
        interleave(pair_prep(0, 0, 0), None)
        for pi, (g, tt) in enumerate(pairs):
            nxt_gen = None
            if pi + 1 < len(pairs):
                g2, tt2 = pairs[pi + 1]
                if tt2 == 0:
                    group_stage(g2)
                nxt_gen = pair_prep(g2, tt2, (pi + 1) % 2)
            interleave(nxt_gen, pair_rec(g, tt, pi % 2))
            if tt == 3:
                P.dma("sp", ynT_scr[:, :, g * 512:(g + 1) * 512], ynT_st[:], r=[("ynT_st", t_) for t_ in range(4)],
                      w=[("ynT_scr", g)])
        P.emit()

    def kcv(w_ap):
        return w_ap.rearrange("(c p) n -> p c n", p=128)

    def rms_rstd(ssv, outv, nfeat, keyr, keyw):
        P.op("act", lambda e: e.activation(out=outv, in_=ssv, func=AF.Ln, bias=epsc[:], scale=1.0 / nfeat),
             r=list(keyr) + ["epsc"], w=keyw)
        P.op("act", lambda e: e.activation(out=outv, in_=outv, func=AF.Exp, scale=-0.5), r=keyw, w=keyw)

    if "p4" in phases:
      with ExitStack() as ph:
        def sbp(name, shape, dt=F32):
            return ph.enter_context(nc.sbuf_tensor(name, list(shape), dt))
        Wa = sbp("Wa", [128, 8, D], BF16); Wb = sbp("Wb", [128, 8, D], BF16); Wo = sbp("Wo", [128, 8, D], BF16)
        Wga = sbp("Wga", [128, 8, D], BF16); Wgb = sbp("Wgb", [128, 8, D], BF16); Wz = sbp("Wz", [128, 8, D], BF16)
        gpm = sbp("gpm", [128, D])
        uTg = [sbp(f"a_uTg{i}", [128, 8, 512], BF16) for i in range(2)]
        yaTg = [sbp("a_yaTg0", [128, 8, 512], BF16)] * 2
        ynTg = [sbp("a_ynTg0", [128, 8, 512], BF16)] * 2
        ybT = sbp("a_ybT", [128, 8, 512], BF16)
        mgT = sbp("a_mgT", [128, 8, 512], BF16)
        zs = [sbp(f"a_zs{i}", [128, 512], BF16) for i in range(2)]
        sga = [sbp(f"a_sga{i}", [128, 512]) for i in range(2)]
        sgb = [sbp(f"a_sgb{i}", [128, 512]) for i in range(2)]
        m1 = [sbp("a_m10", [128, 512])] * 2
        m2 = [sbp("a_m20", [128, 512])] * 2
        o_sb = [sbp(f"a_osb{i}", [128, D]) for i in range(2)]
        xtl = [sbp(f"a_x{i}", [128, D]) for i in range(2)]
        junk4 = sbp("a_junk", [128, 512])
        ss4 = sbp("a_ss", [128, 4]); rs4 = sbp("a_rs", [128, 2]); sst = sbp("a_sst", [128, 2])
        Wp = sbp("Wp", [128, 2, D], BF16); gpl = sbp("gpl", [128, D])
        pt = [sbp(f"a_pt{i}", [128, PLE]) for i in range(2)]
        ptb = sbp("a_ptb", [128, PLE], BF16); pT = sbp("a_pT", [128, 2, 128], BF16)
        e_sb = [sbp("a_esb0", [128, D])] * 2
        P.dma("pool", Wp[:], kcv(w_ple), w=["Wp"])
        P.dma("sp", gpl[:], pbc(ln_ple, D), w=["gpl"])
        for j in range(2):
            sl = slice(j * 512, (j + 1) * 512)
            P.dma("pool", Wa[:, :, sl], kcv(w_proj_a)[:, :, sl], w=[("Wa", j)])
            P.dma("pool", Wb[:, :, sl], kcv(w_proj_b)[:, :, sl], w=[("Wb", j)])
            P.dma("pool", Wo[:, :, sl], kcv(w_out)[:, :, sl], w=[("Wo", j)])
            P.dma("pool", Wga[:, :, sl], w_in_v[:, :, 7184 + j * 512:7184 + (j + 1) * 512], w=[("Wga", j)])
            P.dma("pool", Wgb[:, :, sl], w_in_v[:, :, 8208 + j * 512:8208 + (j + 1) * 512], w=[("Wgb", j)])
            P.dma("pool", Wz[:, :, sl], w_in_v[:, :, 6144 + j * 512:6144 + (j + 1) * 512], w=[("Wz", j)])
        P.dma("sp", gpm[:], pbc(ln_post_mix, D), w=["gpm"])
        rotA = [0]
        pend_a = []

        def nba():
            b = rotA[0] % 8
            rotA[0] += 1
            return b

        for g in range(NG):
            i2 = g % 2
            gs = slice(g * 512, (g + 1) * 512)
            P.dma("sp", uTg[i2][:], uT_scr[:, :, gs], r=[("uT_scr", g)], w=[("a_uTg", i2)])
            P.dma("sp", yaTg[i2][:], yaT_scr.rearrange("h p s -> p h s")[:, :, gs], r=[("yaT_scr", h) for h in range(NH)],
                  w=[("a_yaTg", 0)])
            P.dma("sp", ynTg[i2][:], ynT_scr[:, :, gs], r=[("ynT_scr", g)], w=[("a_ynTg", 0)])
            for fc in range(8):
                j2 = fc % 2
                bk = nba()
                for kc in range(8):
                    mm(banks[bk][:, :], Wz[:, kc, fc * 128:(fc + 1) * 128], uTg[i2][:, kc, :],
                       r=[("Wz", fc // 4), ("a_uTg", i2)], w=[("ps", bk)], start=(kc == 0), stop=(kc == 7))
                P.op("act", lambda e, j2=j2, bk=bk: e.activation(out=zs[j2][:], in_=banks[bk][:, :], func=AF.Silu),
                     r=[("ps", bk)], w=[("a_zs", j2)])
                tt_op("dve", ybT[:, fc, :], ynTg[i2][:, fc, :], zs[j2][:], ALU.mult, r=[("a_ynTg", 0), ("a_zs", j2)],
                      w=[("a_ybT", fc)])
            for fo in range(8):
                j2 = fo % 2
                fsl = slice(fo * 128, (fo + 1) * 128)
                bA, bB, bC, bD = nba(), nba(), nba(), nba()
                for (bk, W, wn, src, sk) in ((bA, Wa, "Wa", yaTg[i2], [("a_yaTg", 0)]),
                                             (bB, Wb, "Wb", ybT, [("a_ybT", c_) for c_ in range(8)]),
                                             (bC, Wga, "Wga", uTg[i2], [("a_uTg", i2)]),
                                             (bD, Wgb, "Wgb", uTg[i2], [("a_uTg", i2)])):
                    for kc in range(8):
                        mm(banks[bk][:, :], W[:, kc, fsl], src[:, kc, :], r=[(wn, fo // 4)] + sk, w=[("ps", bk)],
                           start=(kc == 0), stop=(kc == 7))
                P.op("act", lambda e, j2=j2, bC=bC: e.activation(out=sga[j2][:], in_=banks[bC][:, :], func=AF.Sigmoid),
                     r=[("ps", bC)], w=[("a_sga", j2)])
                P.op("act", lambda e, j2=j2, bD=bD: e.activation(out=sgb[j2][:], in_=banks[bD][:, :], func=AF.Sigmoid),
                     r=[("ps", bD)], w=[("a_sgb", j2)])
                tt_op("dve", m1[j2][:], banks[bA][:, :], sga[j2][:], ALU.mult, r=[("ps", bA), ("a_sga", j2)], w=[("a_m1", 0)])
                tt_op("dve", m2[j2][:], banks[bB][:, :], sgb[j2][:], ALU.mult, r=[("ps", bB), ("a_sgb", j2)], w=[("a_m2", 0)])
                tt_op("pool", mgT[:, fo, :], m1[j2][:], m2[j2][:], ALU.add, r=[("a_m1", 0), ("a_m2", 0)], w=[("a_mgT", fo)])
            for tt in range(4):
                t = g * 4 + tt
                j2 = t % 2
                P.dma("sp", xtl[j2][:], x[t * 128:(t + 1) * 128, :], w=[("a_x", j2)])
                P.dma("sp", pt[j2][:], p_in[t * 128:(t + 1) * 128, :], w=[("a_pt", j2)])
                while pend_a:
                    pend_a.pop(0)()
                P.op("dve", lambda e, j2=j2: e.tensor_copy(out=ptb[:], in_=pt[j2][:]), r=[("a_pt", j2)], w=["a_ptb"])
                for hf in range(2):
                    bk = nba()
                    for kc in range(8):
                        mm(banks[bk][:, :], mgT[:, kc, tt * 128:(tt + 1) * 128], Wo[:, kc, hf * 512:(hf + 1) * 512],
                           r=[("a_mgT", kc), ("Wo", hf)], w=[("ps", bk)], start=(kc == 0), stop=(kc == 7))
                    ev(o_sb[j2][:, hf * 512:(hf + 1) * 512], banks[bk][:, :], r=[("ps", bk)], w=[("a_osb", j2, hf)], eng="dve")
                    P.op("act", lambda e, bk=bk, hf=hf: e.activation(out=junk4[:], in_=banks[bk][:, :], func=AF.Square,
                                                                      accum_out=ss4[:, hf:hf + 1]),
                         r=[("ps", bk)], w=[("a_ss", hf), "a_junk"])
                bk = nba()
                pv = banks[bk][:, 0:128].bitcast(BF16)
                for c in range(2):
                    tr(pv[:, c * 128:(c + 1) * 128], ptb[:, c * 128:(c + 1) * 128], r=["a_ptb"], w=[("ps", bk)])
                ev(pT[:], pv.rearrange("p (c j) -> p c j", c=2), r=[("ps", bk)], w=["a_pT"], eng="dve")
                for hf in range(2):
                    bk = nba()
                    for kc in range(2):
                        mm(banks[bk][:, :], pT[:, kc, :], Wp[:, kc, hf * 512:(hf + 1) * 512], r=["a_pT", "Wp"],
                           w=[("ps", bk)], start=(kc == 0), stop=(kc == 1))
                    ev(e_sb[j2][:, hf * 512:(hf + 1) * 512], banks[bk][:, :], r=[("ps", bk)], w=[("a_esb", 0, hf)], eng="dve")
                    P.op("act", lambda e, bk=bk, hf=hf: e.activation(out=junk4[:], in_=banks[bk][:, :], func=AF.Square,
                                                                      accum_out=ss4[:, 2 + hf:3 + hf]),
                         r=[("ps", bk)], w=[("a_ss", 2 + hf), "a_junk"])
                tt_op("dve", sst[:, 0:1], ss4[:, 0:1], ss4[:, 1:2], ALU.add, r=[("a_ss", 0), ("a_ss", 1)], w=["a_sst"])
                rms_rstd(sst[:, 0:1], rs4[:, 0:1], D, ["a_sst"], ["a_rs"])
                P.op("dve", lambda e, j2=j2: e.scalar_tensor_tensor(out=o_sb[j2][:], in0=o_sb[j2][:], scalar=rs4[:, 0:1],
                                                                    in1=gpm[:], op0=ALU.mult, op1=ALU.mult),
                     r=[("a_osb", j2, 0), ("a_osb", j2, 1), "a_rs", "gpm"], w=[("a_osb", j2, 0), ("a_osb", j2, 1)])
                tt_op("pool", o_sb[j2][:], o_sb[j2][:], xtl[j2][:], ALU.add,
                      r=[("a_osb", j2, 0), ("a_osb", j2, 1), ("a_x", j2)], w=[("a_osb", j2, 0), ("a_osb", j2, 1)])
                pend_a.append(lambda t=t, j2=j2: P.dma("sp", h1_scr[t * 128:(t + 1) * 128, :], o_sb[j2][:],
                                                      r=[("a_osb", j2, 0), ("a_osb", j2, 1)], w=[("h1_scr", t)]))
                tt_op("dve", sst[:, 1:2], ss4[:, 2:3], ss4[:, 3:4], ALU.add, r=[("a_ss", 2), ("a_ss", 3)], w=["a_sst1"])
                rms_rstd(sst[:, 1:2], rs4[:, 1:2], D, ["a_sst1"], ["a_rs1"])
                P.op("dve", lambda e, j2=j2: e.scalar_tensor_tensor(out=e_sb[j2][:], in0=e_sb[j2][:], scalar=rs4[:, 1:2],
                                                                    in1=gpl[:], op0=ALU.mult, op1=ALU.mult),
                     r=[("a_esb", 0, 0), ("a_esb", 0, 1), "a_rs1", "gpl"], w=[("a_esb", 0, 0), ("a_esb", 0, 1)])
                pend_a.append(lambda t=t, j2=j2: P.dma("sp", e_scr[t * 128:(t + 1) * 128, :], e_sb[j2][:],
                                                      r=[("a_esb", 0, 0), ("a_esb", 0, 1)], w=[("e_scr", t)]))
        while pend_a:
            pend_a.pop(0)()
        P.emit()

    if "p4" in phases:
      with ExitStack() as ph:
        def sbp(name, shape, dt=F32):
            return ph.enter_context(nc.sbuf_tensor(name, list(shape), dt))
        GT = 256
        Wg = sbp("Wg", [128, 8, DFF], BF16); Wu = sbp("Wu", [128, 8, DFF], BF16); Wd = sbp("Wd", [128, 22, D], BF16)
        Wpg = sbp("Wpg", [128, 8, D], BF16)
        gff = sbp("gff", [128, D]); gT2 = sbp("gT2", [128, 8])
        fT = [sbp(f"b_fT{i}", [128, 8, GT], BF16) for i in range(2)]
        hT = sbp("b_hT", [128, 22, GT], BF16)
        h1t = [sbp(f"b_h1t{i}", [128, D]) for i in range(4)]
        fbf = sbp("b_fbf", [128, D], BF16)
        sgf = [sbp(f"b_sgf{i}", [128, GT], BF16) for i in range(2)]
        ffs = [sbp(f"b_ffs{i}", [128, D]) for i in range(2)]
        h2b = sbp("b_h2b", [128, D], BF16)
        h2T = sbp("b_h2T", [128, 8, 128], BF16)
        sgp = sbp("b_sgp", [128, D], BF16)
        junk5 = sbp("b_junk", [128, D], BF16)
        ssb = sbp("b_ss", [128, 4]); rsb = sbp("b_rs", [128, 4]); ssq = sbp("b_ssq", [128, 4])
        for kc in range(8):
            P.dma("pool", Wg[:, kc, :], kcv(w_ffn_gate)[:, kc, :], w=[("Wg", kc)])
            P.dma("pool", Wu[:, kc, :], kcv(w_ffn_up)[:, kc, :], w=[("Wu", kc)])
        for c2 in range(11):
            P.dma("pool", Wd[:, 2 * c2:2 * c2 + 2, :], kcv(w_ffn_down)[:, 2 * c2:2 * c2 + 2, :], w=[("Wd", c2)])
        for j in range(2):
            sl = slice(j * 512, (j + 1) * 512)
            P.dma("pool", Wpg[:, :, sl], kcv(w_ple_gate)[:, :, sl], w=[("Wpg", j)])
        P.dma("sp", gff[:], pbc(ln_post_ffn, D), w=["gff"])
        P.dma("sp", gT2[:], ln_pre_ffn.rearrange("(c p) -> p c", p=128), w=["gT2"])
        rotB = [0]

        def nbb():
            b = rotB[0] % 8
            rotB[0] += 1
            return b

        NG2 = S // GT
        pend_b = []

        def prenorm(g):
            for tt in range(2):
                t = g * 2 + tt
                j4 = t % 4
                P.dma("sp", h1t[j4][:], h1_scr[t * 128:(t + 1) * 128, :], r=[("h1_scr", t)], w=[("b_h1t", j4)])
                P.op("act", lambda e, j4=j4: e.activation(out=junk5[:], in_=h1t[j4][:], func=AF.Square, accum_out=ssb[:, 0:1]),
                     r=[("b_h1t", j4)], w=[("b_ss", 0), "b_junk"])
                rms_rstd(ssb[:, 0:1], rsb[:, 0:1], D, [("b_ss", 0)], [("b_rs", 0)])
                P.op("dve", lambda e, j4=j4: e.tensor_scalar(out=fbf[:], in0=h1t[j4][:], scalar1=rsb[:, 0:1], scalar2=None,
                                                           op0=ALU.mult),
                     r=[("b_h1t", j4), ("b_rs", 0)], w=["b_fbf"])
                bk = nbb()
                pv = banks[bk][:, :].bitcast(BF16)
                for c in range(8):
                    tr(pv[:, c * 128:(c + 1) * 128], fbf[:, c * 128:(c + 1) * 128], r=["b_fbf"], w=[("ps", bk)])
                tt_op("dve", fT[g % 2][:, :, tt * 128:(tt + 1) * 128], pv.rearrange("p (c j) -> p c j", c=8),
                      bcast_ap(gT2[:], [[1, 8], [0, 128]]), ALU.mult, r=[("ps", bk), "gT2"], w=[("b_fT", g % 2, tt)])

        def gateup(g):
            fTg = fT[g % 2]
            fT_keys = [("b_fT", g % 2, tt) for tt in range(2)]
            for fc in range(22):
                j2 = fc % 2
                fsl = slice(fc * 128, (fc + 1) * 128)
                bG, bU = nbb(), nbb()
                for kc in range(8):
                    mm(banks[bG][:, 0:GT], Wg[:, kc, fsl], fTg[:, kc, :], r=[("Wg", kc)] + fT_keys, w=[("ps", bG)],
                       start=(kc == 0), stop=(kc == 7))
                for kc in range(8):
                    mm(banks[bU][:, 0:GT], Wu[:, kc, fsl], fTg[:, kc, :], r=[("Wu", kc)] + fT_keys, w=[("ps", bU)],
                       start=(kc == 0), stop=(kc == 7))
                P.op("act", lambda e, j2=j2, bG=bG: e.activation(out=sgf[j2][:], in_=banks[bG][:, 0:GT], func=AF.Silu),
                     r=[("ps", bG)], w=[("b_sgf", j2)])
                tt_op("dve", hT[:, fc, :], sgf[j2][:], banks[bU][:, 0:GT], ALU.mult, r=[("b_sgf", j2), ("ps", bU)],
                      w=[("b_hT", fc)])

        def down(g, tt):
            t = g * 2 + tt
            fb = ffs[tt]
            for hf in range(2):
                bk = nbb()
                for fc in range(22):
                    mm(banks[bk][:, :], hT[:, fc, tt * 128:(tt + 1) * 128], Wd[:, fc, hf * 512:(hf + 1) * 512],
                       r=[("b_hT", fc), ("Wd", fc // 2)], w=[("ps", bk)], start=(fc == 0), stop=(fc == 21))
                ev(fb[:, hf * 512:(hf + 1) * 512], banks[bk][:, :], r=[("ps", bk)], w=[("b_ffs", tt, hf)], eng="dve")
                P.op("act", lambda e, bk=bk, hf=hf, tt=tt: e.activation(out=junk5[:, 0:512], in_=banks[bk][:, :], func=AF.Square,
                                                                         accum_out=ssq[:, 2 * tt + hf:2 * tt + hf + 1]),
                     r=[("ps", bk)], w=[("b_ssq", tt, hf), "b_junk"])

        def tail(g, tt):
            t = g * 2 + tt
            j4 = t % 4
            fb = ffs[tt]
            fk = [("b_ffs", tt, 0), ("b_ffs", tt, 1)]
            tt_op("dve", ssb[:, 1 + tt:2 + tt], ssq[:, 2 * tt:2 * tt + 1], ssq[:, 2 * tt + 1:2 * tt + 2], ALU.add,
                  r=[("b_ssq", tt, 0), ("b_ssq", tt, 1)], w=[("b_ss", 1 + tt)])
            rms_rstd(ssb[:, 1 + tt:2 + tt], rsb[:, 1 + tt:2 + tt], D, [("b_ss", 1 + tt)], [("b_rs", 1 + tt)])
            P.op("dve", lambda e: e.scalar_tensor_tensor(out=fb[:], in0=fb[:], scalar=rsb[:, 1 + tt:2 + tt], in1=gff[:],
                                                         op0=ALU.mult, op1=ALU.mult),
                 r=fk + [("b_rs", 1 + tt), "gff"], w=fk)
            tt_op("pool", h1t[j4][:], h1t[j4][:], fb[:], ALU.add, r=[("b_h1t", j4)] + fk, w=[("b_h1t", j4)])
            P.dma("sp", fb[:], e_scr[t * 128:(t + 1) * 128, :], r=[("e_scr", t)], w=fk)
            P.op("act", lambda e: e.activation(out=h2b[:], in_=h1t[j4][:], func=AF.Copy), r=[("b_h1t", j4)], w=["b_h2b"])
            bk = nbb()
            pv = banks[bk][:, :].bitcast(BF16)
            for c in range(8):
                tr(pv[:, c * 128:(c + 1) * 128], h2b[:, c * 128:(c + 1) * 128], r=["b_h2b"], w=[("ps", bk)])
            ev(h2T[:], pv.rearrange("p (c j) -> p c j", c=8), r=[("ps", bk)], w=["b_h2T"], eng="dve")
            for hf in range(2):
                bk = nbb()
                for kc in range(8):
                    mm(banks[bk][:, :], h2T[:, kc, :], Wpg[:, kc, hf * 512:(hf + 1) * 512], r=["b_h2T", ("Wpg", hf)],
                       w=[("ps", bk)], start=(kc == 0), stop=(kc == 7))
                P.op("act", lambda e, bk=bk, hf=hf: e.activation(out=sgp[:, hf * 512:(hf + 1) * 512], in_=banks[bk][:, :],
                                                                  func=AF.Sigmoid),
                     r=[("ps", bk)], w=[("b_sgp", hf)])
            tt_op("pool", fb[:], fb[:], sgp[:], ALU.mult, r=fk + [("b_sgp", 0), ("b_sgp", 1)], w=fk)
            tt_op("pool", fb[:], fb[:], h1t[j4][:], ALU.add, r=fk + [("b_h1t", j4)], w=fk)
            P.dma("sp", out[t * 128:(t + 1) * 128, :], fb[:], r=fk, w=[("out", t)])

        prenorm(0)
        for g in range(NG2):
            gateup(g)
            if g + 1 < NG2:
                prenorm(g + 1)
            down(g, 0)
            down(g, 1)
            tail(g, 0)
            tail(g, 1)
        P.finish()
        P.emit()
    else:
        P.finish()
        P.emit()
    es.close()
    return nc


_WNAMES = ("ln_pre_mix", "w_in", "conv_w", "a_log", "dt_bias", "gdn_norm", "w_proj_a", "w_proj_b", "w_out",
           "ln_post_mix", "ln_pre_ffn", "w_ffn_gate", "w_ffn_up", "w_ffn_down", "ln_post_ffn", "w_ple", "ln_ple",
           "w_ple_gate")


def make_in_maps(inputs, S, n_cores):
    consts = host_consts(S)
    shared = {k: np.ascontiguousarray(np.asarray(inputs[k], dtype=np.float32)[0]) for k in _WNAMES}
    x = np.asarray(inputs["x"], dtype=np.float32)
    p = np.asarray(inputs["p"], dtype=np.float32)
    maps = []
    for i in range(n_cores):
        m = dict(shared)
        m.update(consts)
        m["x"] = np.ascontiguousarray(x[i])
        m["p"] = np.ascontiguousarray(p[0, i])
        maps.append(m)
    return maps


def kernel(**inputs):
    x = np.asarray(inputs["x"])
    B, S, _ = x.shape
    nc = build(S)
    in_maps = make_in_maps(inputs, S, B)
    res = run_bass_kernel_spmd(nc, in_maps, core_ids=list(range(B)))
    return np.stack([np.asarray(r["out"], dtype=np.float32) for r in res.results], axis=0)
```

```python
import numpy as np
from contextlib import ExitStack
import concourse.bass as bass
import concourse.mybir as mybir
from concourse.bass_utils import run_bass_kernel_spmd

F32 = mybir.dt.float32
BF16 = mybir.dt.bfloat16
ALU = mybir.AluOpType
AF = mybir.ActivationFunctionType
AX = mybir.AxisListType

D = 1024
HD = 128
NH = 8
DFF = 2816
PLE = 256
INW = 9232
EPS = 1e-6
NEG = -30000.0


class Prog:
    ENG = ("pe", "act", "dve", "pool", "sp")

    def __init__(self, nc, es, n_dma_sems=32):
        self.nc = nc
        self.ops = {e: [] for e in self.ENG}
        self.lastw = {}
        self.readers = {}
        self.sem = {e: es.enter_context(nc.semaphore("s_" + e)) for e in ("pe", "act", "dve", "pool")}
        self.dsems = [es.enter_context(nc.semaphore(f"dq{i}")) for i in range(n_dma_sems)]
        self.dcount = [0] * n_dma_sems
        self.dlast = [None] * n_dma_sems
        self.ndma = 0
        self.npool = 0

    @staticmethod
    def _is_psum(k):
        return isinstance(k, tuple) and k[0] == "ps"

    def _deps(self, eng, r, w, is_dma):
        strong = set()
        weak = set()
        for k in r:
            t = self.lastw.get(k)
            if t is not None:
                strong.add(t)
            if self._is_psum(k):
                for t2 in self.readers.get(k, ()):
                    weak.add(t2)
        for k in w:
            t = self.lastw.get(k)
            if t is not None:
                strong.add(t)
            for t2 in self.readers.get(k, ()):
                weak.add(t2)
        deps = set()
        for t in strong:
            if t[0] == "c" and t[1] == eng and eng == "pe":
                continue
            deps.add(t)
        for t in weak:
            if t[0] == "c" and t[1] == eng and not is_dma:
                continue
            deps.add(t)
        return deps

    def _mark(self, deps):
        done = getattr(self, "done", None)
        for t in deps:
            if t[0] == "c" and (done is None or t[2] >= done[t[1]]):
                self.ops[t[1]][t[2]]["signal"] = True

    def _commit(self, tok, r, w):
        for k in w:
            self.lastw[k] = tok
            self.readers[k] = []
        for k in r:
            self.readers.setdefault(k, []).append(tok)

    def op(self, eng, fn, r=(), w=()):
        deps = self._deps(eng, r, w, False)
        self._mark(deps)
        idx = len(self.ops[eng])
        self.ops[eng].append(dict(fn=fn, deps=deps, signal=False, dma=None))
        tok = ("c", eng, idx)
        self._commit(tok, r, w)
        return tok

    def dma(self, q, out, in_, r=(), w=(), **kw):
        deps = self._deps(q, r, w, True)
        if q == "pool":
            s = self.npool % 8
            self.npool += 1
        else:
            s = 8 + self.ndma % (len(self.dsems) - 8)
            self.ndma += 1
        if self.dlast[s] is not None:
            deps.add(self.dlast[s])
        self._mark(deps)
        self.dcount[s] += 1
        tok = ("d", s, 16 * self.dcount[s])
        self.dlast[s] = tok
        fn = lambda e, out=out, in_=in_, kw=kw: e.dma_start(out=out, in_=in_, **kw)
        self.ops[q].append(dict(fn=fn, deps=deps, signal=False, dma=s))
        self._commit(tok, r, w)
        return tok

    def finish(self, q="sp"):
        deps = set(t for t in self.dlast if t is not None)
        self.ops[q].append(dict(fn=None, deps=deps, signal=False, dma=None))

    def emit(self):
        nc = self.nc
        self.finish()
        if not hasattr(self, "done"):
            self.done = {e: 0 for e in self.ENG}
            self.sigcnt = {e: [] for e in self.ENG}
            self.waited = {e: {} for e in self.ENG}
        for e in ("pe", "act", "dve", "pool"):
            for o in reversed(self.ops[e][self.done[e]:]):
                if o["fn"] is not None and o["dma"] is None:
                    o["signal"] = True
                    break
        for e in ("pe", "act", "dve", "pool"):
            arr = self.sigcnt[e]
            c = arr[-1] if arr else 0
            for o in self.ops[e][self.done[e]:]:
                if o["signal"] and o["dma"] is None and o["fn"] is not None:
                    c += 1
                arr.append(c)

        def sigval(e, idx):
            ops = self.ops[e]
            j = idx
            while not (ops[j]["signal"] and ops[j]["dma"] is None and ops[j]["fn"] is not None):
                j += 1
            return self.sigcnt[e][j]

        def run(ename, eng):
            waited = self.waited[ename]
            for o in self.ops[ename][self.done[ename]:]:
                need = {}
                for t in o["deps"]:
                    if t[0] == "c":
                        key = ("c", t[1])
                        val = sigval(t[1], t[2])
                    else:
                        key = ("d", t[1])
                        val = t[2]
                    if need.get(key, 0) < val:
                        need[key] = val
                for key, val in need.items():
                    if waited.get(key, 0) >= val:
                        continue
                    sh = self.sem[key[1]] if key[0] == "c" else self.dsems[key[1]]
                    eng.wait_ge(sh, val)
                    waited[key] = val
                if o["fn"] is None:
                    continue
                ins = o["fn"](eng)
                if o["dma"] is not None:
                    ins.then_inc(self.dsems[o["dma"]], 16)
                elif o["signal"]:
                    ins.then_inc(self.sem[ename], 1)

        with nc.allow_non_contiguous_dma(reason="small strided constant loads"), nc.Block() as block:
            @block.tensor
            def _(e):
                run("pe", e)

            @block.scalar
            def _(e):
                run("act", e)

            @block.vector
            def _(e):
                run("dve", e)

            @block.gpsimd
            def _(e):
                run("pool", e)

            @block.sync
            def _(e):
                run("sp", e)
        for e in self.ENG:
            self.done[e] = len(self.ops[e])


def bcast_ap(ap, dims):
    return bass.AP(ap.tensor, ap.offset, [list(ap.ap[0])] + [list(d) for d in dims])


def host_consts(S):
    import ml_dtypes
    bf = ml_dtypes.bfloat16
    c = {}
    c["ident"] = np.eye(128, dtype=np.float32)
    inv = (1.0 / (np.float32(10000.0) ** (np.arange(0, HD, 2, dtype=np.float32) / np.float32(HD)))).astype(np.float32)
    ang = (np.arange(S, dtype=np.float32)[:, None] * inv[None, :]).astype(np.float32)
    cos = np.cos(ang).astype(np.float32).T
    sin = np.sin(ang).astype(np.float32).T
    c["ropecos"] = np.ascontiguousarray(np.concatenate([cos, cos], 0))
    c["ropesin"] = np.ascontiguousarray(np.concatenate([-sin, sin], 0))
    key = np.arange(128)[:, None, None] + 128 * np.arange(2)[None, :, None]
    qq = np.arange(256)[None, None, :]
    c["caus"] = np.where(key <= qq, 0.0, NEG).astype(bf)
    sel = np.zeros((128, 16, 128), np.float32)
    for kb in range(16):
        sel[kb, kb, :] = 1.0
    c["sel"] = sel.astype(bf)
    inval = np.zeros((128, 16, 16), np.float32)
    for qb in range(16):
        inval[:, qb, qb:] = -1e30
    c["inval"] = inval
    p = np.arange(128)
    same = (p[:, None] // 64) == (p[None, :] // 64)
    c["gM1"] = (same & (p[:, None] <= p[None, :])).astype(np.float32)
    c["gM2"] = (same & (p[:, None] > p[None, :])).astype(np.float32)
    mc = np.zeros((128, 2, 128), np.float32)
    mc[0:64, 0, :] = 1.0
    mc[64:128, 1, :] = 1.0
    c["gMc"] = mc
    c["gcausneg"] = np.where(same & (p[None, :] <= p[:, None]), 0.0, NEG).astype(bf)
    c["gstrict"] = (same & (p[None, :] < p[:, None])).astype(np.float32).astype(bf)
    rm = np.zeros((128, 2), np.float32)
    rm[0:64, 0] = 1.0
    rm[64:128, 1] = 1.0
    c["growmask"] = rm
    return c


def build(S, debug=(), heads=tuple(range(NH)), phases=("p1", "p2", "p3", "p4")):
    T = S // 128
    NG = S // 512
    NB = S // 256
    nc = bass.Bass("TRN2", target_bir_lowering=False)
    es = ExitStack()

    def dram_in(name, shape, dt=F32):
        return nc.dram_tensor(name, list(shape), dt, kind="ExternalInput").ap()

    x = dram_in("x", [S, D])
    ln_pre_mix = dram_in("ln_pre_mix", [D])
    w_in = dram_in("w_in", [D, INW])
    ident_in = dram_in("ident", [128, 128])
    ropecos = dram_in("ropecos", [128, S])
    ropesin = dram_in("ropesin", [128, S])
    caus_in = dram_in("caus", [128, 2, 256], BF16)
    sel_in = dram_in("sel", [128, 16, 128], BF16)
    inval_in = dram_in("inval", [128, 16, 16])
    conv_w = dram_in("conv_w", [4, 3072])
    a_log = dram_in("a_log", [NH])
    dt_bias = dram_in("dt_bias", [NH])
    gdn_norm = dram_in("gdn_norm", [HD])
    p_in = dram_in("p", [S, PLE])
    w_proj_a = dram_in("w_proj_a", [D, D])
    w_proj_b = dram_in("w_proj_b", [D, D])
    w_out = dram_in("w_out", [D, D])
    ln_post_mix = dram_in("ln_post_mix", [D])
    ln_pre_ffn = dram_in("ln_pre_ffn", [D])
    w_ffn_gate = dram_in("w_ffn_gate", [D, DFF])
    w_ffn_up = dram_in("w_ffn_up", [D, DFF])
    w_ffn_down = dram_in("w_ffn_down", [DFF, D])
    ln_post_ffn = dram_in("ln_post_ffn", [D])
    w_ple = dram_in("w_ple", [PLE, D])
    ln_ple = dram_in("ln_ple", [D])
    w_ple_gate = dram_in("w_ple_gate", [D, D])
    h1_scr = nc.dram_tensor("h1_scr", [S, D], F32, kind="ExternalOutput" if "h1" in debug else "Internal").ap()
    e_scr = nc.dram_tensor("e_scr", [S, D], F32, kind="Internal").ap()
    gM1_in = dram_in("gM1", [128, 128])
    gM2_in = dram_in("gM2", [128, 128])
    gMc_in = dram_in("gMc", [128, 2, 128])
    gcausneg_in = dram_in("gcausneg", [128, 128], BF16)
    gstrict_in = dram_in("gstrict", [128, 128], BF16)
    growmask_in = dram_in("growmask", [128, 2])
    out = nc.dram_tensor("out", [S, D], F32, kind="ExternalOutput").ap()
    uT_scr = nc.dram_tensor("uT_scr", [128, 8, S], BF16, kind="Internal").ap()
    ynT_scr = nc.dram_tensor("ynT_scr", [128, 8, S], BF16,
                             kind="ExternalOutput" if "ynT" in debug else "Internal").ap()
    yaT_scr = nc.dram_tensor("yaT_scr", [NH, 128, S], BF16,
                             kind="ExternalOutput" if "yaT" in debug else "Internal").ap()
    dbg = {}
    if "uT" in debug:
        dbg["uT"] = nc.dram_tensor("dbg_uT", [128, 8, S], BF16, kind="ExternalOutput").ap()

    def sb(name, shape, dt=F32):
        return es.enter_context(nc.sbuf_tensor(name, list(shape), dt))

    def ps(name):
        return es.enter_context(nc.psum_tensor(name, [128, 512], F32))

    P = Prog(nc, es)
    banks = [ps(f"bank{i}") for i in range(8)]
    w_in_v = w_in.rearrange("(c p) n -> p c n", p=128)

    ident_f = sb("ident_f", [128, 128])
    ident_b = sb("ident_b", [128, 128], BF16)
    gainT = sb("gainT", [128, 8])
    epsc = sb("epsc", [128, 1])
    es12 = ExitStack()
    uT = es12.enter_context(nc.sbuf_tensor("uT", [128, 8, S], BF16))
    P.dma("sp", ident_f[:], ident_in, w=["ident_f"])
    P.dma("sp", gainT[:], ln_pre_mix.rearrange("(c p) -> p c", p=128), w=["gainT"])
    P.op("dve", lambda e: e.tensor_copy(out=ident_b[:], in_=ident_f[:]), r=["ident_f"], w=["ident_b"])
    P.op("pool", lambda e: e.memset(epsc[:], EPS), w=["epsc"])

    with ExitStack() as ph:
        def sbp(name, shape, dt=F32):
            return ph.enter_context(nc.sbuf_tensor(name, list(shape), dt))
        xt = [sbp(f"xt{i}", [128, D]) for i in range(3)]
        xs = [sbp(f"xs{i}", [128, D], BF16) for i in range(2)]
        junk = sbp("junk", [128, D])
        ss = sbp("ss", [128, T])
        rstd = sbp("rstd", [128, T])
        lnv = sbp("lnv", [128, T])
        for t in range(T):
            xb = xt[t % 3]
            P.dma("sp", xb[:], x[t * 128:(t + 1) * 128, :], w=[("xt", t % 3)])
            P.op("act", lambda e, xb=xb, t=t: e.activation(out=junk[:], in_=xb[:], func=AF.Square,
                                                            accum_out=ss[:, t:t + 1]),
                 r=[("xt", t % 3)], w=[("ss", t), "junk_p1"])
            P.op("act", lambda e, t=t: e.activation(out=lnv[:, t:t + 1], in_=ss[:, t:t + 1], func=AF.Ln,
                                                    bias=epsc[:], scale=1.0 / D),
                 r=[("ss", t), "epsc"], w=[("lnv", t)])
            P.op("act", lambda e, t=t: e.activation(out=rstd[:, t:t + 1], in_=lnv[:, t:t + 1], func=AF.Exp,
                                                    scale=-0.5),
                 r=[("lnv", t)], w=[("rstd", t)])
            xsb = xs[t % 2]
            P.op("dve", lambda e, xb=xb, xsb=xsb, t=t: e.tensor_scalar(out=xsb[:], in0=xb[:],
                                                                       scalar1=rstd[:, t:t + 1], scalar2=None,
                                                                       op0=ALU.mult),
                 r=[("xt", t % 3), ("rstd", t)], w=[("xs", t % 2)])
            bk = t % 2
            pT = banks[bk][:, 0:512].bitcast(BF16)
            for c in range(8):
                P.op("pe", lambda e, pT=pT, xsb=xsb, c=c: e.transpose(out=pT[:, c * 128:(c + 1) * 128],
                                                                      in_=xsb[:, c * 128:(c + 1) * 128],
                                                                      identity=ident_b[:]),
                     r=[("xs", t % 2), "ident_b"], w=[("ps", bk)])
            P.op("dve", lambda e, pT=pT, t=t: e.tensor_tensor(
                out=uT[:, :, t * 128:(t + 1) * 128],
                in0=pT.rearrange("p (c j) -> p c j", c=8),
                in1=bcast_ap(gainT[:], [[1, 8], [0, 128]]),
                op=ALU.mult),
                r=[("ps", bk), "gainT"], w=[("uT", t)])
        if "uT" in dbg:
            P.dma("sp", dbg["uT"], uT[:], r=[("uT", t) for t in range(T)])
        for g in range(NG):
            P.dma("sp", uT_scr[:, :, g * 512:(g + 1) * 512], uT[:, :, g * 512:(g + 1) * 512],
                  r=[("uT", t) for t in range(4 * g, 4 * g + 4)], w=[("uT_scr", g)])
        P.emit()

    uT_keys = lambda g: [("uT", t) for t in range(4 * g, 4 * g + 4)]

    if "p2" in phases:
      with ExitStack() as ph:
        def sbp(name, shape, dt=F32):
            return ph.enter_context(nc.sbuf_tensor(name, list(shape), dt))
        cosT = sbp("cosT", [128, S])
        sinT = sbp("sinT", [128, S])
        caus = sbp("caus_sb", [128, 2, 256], BF16)
        sel = sbp("sel_sb", [128, 16, 128], BF16)
        inval = sbp("inval_sb", [128, 16, 16])
        P.dma("sp", cosT[:], ropecos, w=["cosT"])
        P.dma("sp", sinT[:], ropesin, w=["sinT"])
        P.dma("sp", caus[:], caus_in, w=["caus"])
        P.dma("sp", sel[:], sel_in, w=["sel"])
        P.dma("sp", inval[:], inval_in, w=["inval"])
        wnames = ("wq", "wqs", "wk", "wks", "wv")
        wset = [{n: sbp(f"{n}{i}", [128, 8, 128], BF16) for n in wnames} for i in range(2)]
        qT = sbp("qT", [128, S], BF16)
        kT = sbp("kT", [128, S], BF16)
        v_sb = sbp("v_sb", [128, T, 129], BF16)
        maskT = sbp("maskT", [128, S], BF16)
        maskp = sbp("maskp", [128, 4, 128], BF16)
        t1 = [sbp(f"t1_{i}", [128, 512]) for i in range(2)]
        t2 = [sbp(f"t2_{i}", [128, 512]) for i in range(2)]
        r32 = [sbp(f"r32_{i}", [128, 512]) for i in range(2)]
        kmean = sbp("kmean", [128, 16])
        gsb = sbp("gsb", [128, 4, 16])
        m8 = sbp("m8", [128, 4, 8])
        PT = [sbp(f"PT{i}", [128, 512], BF16) for i in range(2)]
        rden = sbp("rden", [128, 2])
        yat = [sbp(f"yat{i}", [128, 128], BF16) for i in range(2)]
        yaT_st = [sbp(f"yaTst{i}", [128, S], BF16) for i in range(2)]
        P.op("pool", lambda e: e.memset(v_sb[:, :, 128:129], 1.0), w=["v_ones"])
        P.op("pool", lambda e: e.memset(maskp[:], 0.0), w=["maskp"])
        P.op("pool", lambda e: e.memset(kmean[:], 0.0), w=["kmean"])
        scale = float(HD) ** -0.5

        def load_w(h):
            ws = wset[h % 2]
            i = h % 2
            cq, ck, cv = h * 128, 1024 + h * 128, 2048 + h * 128
            P.dma("pool", ws["wq"][:], w_in_v[:, :, cq:cq + 128], w=[("w", i, "wq")])
            P.dma("pool", ws["wqs"][:, :, 0:64], w_in_v[:, :, cq + 64:cq + 128], w=[("w", i, "wqs0")])
            P.dma("pool", ws["wqs"][:, :, 64:128], w_in_v[:, :, cq:cq + 64], w=[("w", i, "wqs1")])
            P.dma("pool", ws["wk"][:], w_in_v[:, :, ck:ck + 128], w=[("w", i, "wk")])
            P.dma("pool", ws["wks"][:, :, 0:64], w_in_v[:, :, ck + 64:ck + 128], w=[("w", i, "wks0")])
            P.dma("pool", ws["wks"][:, :, 64:128], w_in_v[:, :, ck:ck + 64], w=[("w", i, "wks1")])
            P.dma("pool", ws["wv"][:], w_in_v[:, :, cv:cv + 128], w=[("w", i, "wv")])

        def wkeys(i, n):
            if n in ("wqs", "wks"):
                return [("w", i, n + "0"), ("w", i, n + "1")]
            return [("w", i, n)]

        def rope_proj(h, g, wa, wb, dstT, is_k):
            i = h % 2
            ws = wset[i]
            ba, bb = (0, 1) if g % 2 == 0 else (2, 3)
            for (bk, wn) in ((ba, wa), (bb, wb)):
                for kc in range(8):
                    P.op("pe", lambda e, bk=bk, wn=wn, kc=kc: e.matmul(
                        banks[bk][:, :], lhsT=ws[wn][:, kc, :], rhs=uT[:, kc, g * 512:(g + 1) * 512],
                        start=(kc == 0), stop=(kc == 7)),
                        r=wkeys(i, wn) + uT_keys(g), w=[("ps", bk)])
            j = g % 2
            P.op("dve", lambda e: e.tensor_tensor(out=t1[j][:], in0=banks[ba][:, :], in1=cosT[:, g * 512:(g + 1) * 512],
                                                  op=ALU.mult),
                 r=[("ps", ba), "cosT"], w=[("t1", j)])
            P.op("dve", lambda e: e.tensor_tensor(out=t2[j][:], in0=banks[bb][:, :], in1=sinT[:, g * 512:(g + 1) * 512],
                                                  op=ALU.mult),
                 r=[("ps", bb), "sinT"], w=[("t2", j)])
            P.op("pool", lambda e: e.tensor_tensor(out=r32[j][:], in0=t1[j][:], in1=t2[j][:], op=ALU.add),
                 r=[("t1", j), ("t2", j)], w=[("r32", j)])
            P.op("act", lambda e: e.activation(out=dstT[:, g * 512:(g + 1) * 512], in_=r32[j][:], func=AF.Copy),
                 r=[("r32", j)], w=[("qk", is_k, g)])
            return j

        def do_head(hi, h):
            if hi == 0:
                load_w(h)
            i = h % 2
            ws = wset[i]
            for g in range(NG):
                j = rope_proj(h, g, "wk", "wks", kT, 1)
                P.op("dve", lambda e, j=j, g=g: e.tensor_reduce(
                    out=kmean[:, 2 * g:2 * g + 2], in_=r32[j][:].rearrange("p (b q) -> p b q", q=256),
                    axis=AX.X, op=ALU.add),
                    r=[("r32", j)], w=["kmean"])
            for t in range(T):
                for kc in range(8):
                    P.op("pe", lambda e, t=t, kc=kc, vbk=6 + (t // 4) % 2: e.matmul(
                        banks[vbk][:, (t % 4) * 128:(t % 4 + 1) * 128], lhsT=uT[:, kc, t * 128:(t + 1) * 128],
                        rhs=ws["wv"][:, kc, :], start=(kc == 0), stop=(kc == 7)),
                        r=wkeys(i, "wv") + [("uT", t)], w=[("ps", 6 + (t // 4) % 2)])
                if t % 4 == 3:
                    P.op("act", lambda e, t=t, vbk=6 + (t // 4) % 2: e.activation(
                        out=v_sb[:, t - 3:t + 1, 0:128], in_=banks[vbk][:, :].rearrange("p (a d) -> p a d", d=128),
                        func=AF.Copy),
                        r=[("ps", 6 + (t // 4) % 2)], w=[("v", t // 4)])
            for g in range(NG):
                j = rope_proj(h, g, "wq", "wqs", qT, 0)
                for tt in range(4):
                    P.op("pe", lambda e, j=j, tt=tt: e.matmul(
                        banks[4][:, tt * 16:(tt + 1) * 16], lhsT=r32[j][:, tt * 128:(tt + 1) * 128],
                        rhs=kmean[:, :], start=True, stop=True),
                        r=[("r32", j), "kmean"], w=[("ps", 4)])
                P.op("dve", lambda e, g=g: e.tensor_tensor(
                    out=gsb[:].rearrange("p (b t) c -> p b t c", b=2),
                    in0=banks[4][:, 0:64].rearrange("p (b t c) -> p b t c", b=2, t=2),
                    in1=bcast_ap(inval[:, 2 * g, :], [[16, 2], [0, 2], [1, 16]]),
                    op=ALU.add),
                    r=[("ps", 4), "inval"], w=["gsb"])
                for tt in range(4):
                    P.op("dve", lambda e, tt=tt: e.max(out=m8[:, tt, :], in_=gsb[:, tt, :]),
                         r=["gsb"], w=[("m8", tt)])
                P.op("dve", lambda e: e.tensor_tensor(
                    out=maskp[:, :, 0:16], in0=gsb[:],
                    in1=bcast_ap(m8[:, 0, 2:3], [[8, 4], [0, 16]]), op=ALU.is_lt),
                    r=["gsb"] + [("m8", tt) for tt in range(4)], w=["maskp"])
                pM = banks[5][:, 0:256].bitcast(BF16)
                for tt in range(4):
                    P.op("pe", lambda e, tt=tt, pM=pM: e.transpose(
                        out=pM[:, tt * 128:(tt + 1) * 128], in_=maskp[:, tt, :], identity=ident_b[:]),
                        r=["maskp", "ident_b"], w=[("ps", 5)])
                P.op("act", lambda e, g=g, pM=pM: e.activation(
                    out=maskT[:, g * 512:(g + 1) * 512], in_=pM, func=AF.Copy, scale=NEG),
                    r=[("ps", 5)], w=[("maskT", g)])
            if hi + 1 < len(heads):
                load_w(heads[hi + 1])
            yst = yaT_st[hi % 2]
            steps = [(qb, kb) for qb in range(NB) for kb in range(qb + 1)]

            def emit_scores(i):
                qb, kb = steps[i]
                g = qb // 2
                bs = i % 2
                for kc in range(2):
                    kt = kb * 2 + kc
                    P.op("pe", lambda e, bs=bs, kc=kc, kt=kt, qb=qb: e.matmul(
                        banks[bs][:, kc * 256:(kc + 1) * 256], lhsT=kT[:, kt * 128:(kt + 1) * 128],
                        rhs=qT[:, qb * 256:(qb + 1) * 256], start=True, stop=False),
                        r=[("qk", 1, kt // 4), ("qk", 0, g)], w=[("ps", bs)])
                    if kb < qb:
                        P.op("pe", lambda e, bs=bs, kc=kc, kb=kb, qb=qb: e.matmul(
                            banks[bs][:, kc * 256:(kc + 1) * 256], lhsT=sel[:, kb, :],
                            rhs=maskT[:, qb * 256:(qb + 1) * 256], start=False, stop=True),
                            r=["sel", ("maskT", g)], w=[("ps", bs)])
                    else:
                        P.op("pe", lambda e, bs=bs, kc=kc: e.matmul(
                            banks[bs][:, kc * 256:(kc + 1) * 256], lhsT=ident_b[:],
                            rhs=caus[:, kc, :], start=False, stop=True),
                            r=["ident_b", "caus"], w=[("ps", bs)])
                P.op("act", lambda e, bs=bs: e.activation(out=PT[bs][:], in_=banks[bs][:, :], func=AF.Exp, scale=scale),
                     r=[("ps", bs)], w=[("PT", bs)])

            def emit_pv(i):
                qb, kb = steps[i]
                bs = i % 2
                for kc in range(2):
                    kt = kb * 2 + kc
                    for hf in range(2):
                        P.op("pe", lambda e, bs=bs, kc=kc, kt=kt, hf=hf, kb=kb, qb=qb: e.matmul(
                            banks[2 + hf][:, 0:129], lhsT=PT[bs][:, kc * 256 + hf * 128:kc * 256 + hf * 128 + 128],
                            rhs=v_sb[:, kt, :], start=(kb == 0 and kc == 0), stop=(kb == qb and kc == 1)),
                            r=[("PT", bs), ("v", kt // 4), "v_ones"], w=[("ps", 2 + hf)])
                if kb == qb:
                    pY = banks[5][:, 0:128].bitcast(BF16)
                    for hf in range(2):
                        P.op("dve", lambda e, hf=hf: e.reciprocal(out=rden[:, hf:hf + 1], in_=banks[2 + hf][:, 128:129]),
                             r=[("ps", 2 + hf)], w=[("rden", hf)])
                        P.op("dve", lambda e, hf=hf: e.tensor_scalar(out=yat[hf][:], in0=banks[2 + hf][:, 0:128],
                                                                   scalar1=rden[:, hf:hf + 1], scalar2=None, op0=ALU.mult),
                             r=[("ps", 2 + hf), ("rden", hf)], w=[("yat", hf)])
                    pend_fin.append((qb, pY))

            def emit_fin():
                while pend_fin:
                    qb, pY = pend_fin.pop(0)
                    for hf in range(2):
                        P.op("pe", lambda e, hf=hf, pY=pY: e.transpose(out=pY[:, hf * 128:(hf + 1) * 128], in_=yat[hf][:],
                                                                     identity=ident_b[:]),
                             r=[("yat", hf), "ident_b"], w=[("ps", 5)])
                    P.op("dve", lambda e, qb=qb, pY=pY, yst=yst: e.tensor_copy(out=yst[:, qb * 256:(qb + 1) * 256], in_=pY),
                         r=[("ps", 5)], w=[("yst", hi % 2, qb)])

            pend_fin = []
            emit_scores(0)
            for i in range(len(steps)):
                if i + 1 < len(steps):
                    emit_scores(i + 1)
                emit_fin()
                emit_pv(i)
            emit_fin()
            P.dma("sp", yaT_scr[h], yst[:], r=[("yst", hi % 2, qb) for qb in range(NB)], w=[("yaT_scr", h)])

        for hi, h in enumerate(heads):
            do_head(hi, h)
        P.emit()

    es12.close()

    def mm(out_, lhsT, rhs, r, w, start=True, stop=True):
        P.op("pe", lambda e: e.matmul(out_, lhsT=lhsT, rhs=rhs, start=start, stop=stop), r=r, w=w)

    def tr(out_, in_, r, w):
        P.op("pe", lambda e: e.transpose(out=out_, in_=in_, identity=ident_b[:]), r=list(r) + ["ident_b"], w=w)

    evc = [0]

    def ev(out_, in_, r, w, eng=None):
        if eng is None:
            eng = ("act", "dve", "act")[evc[0] % 3]
            evc[0] += 1
        if eng == "act":
            P.op("act", lambda e: e.activation(out=out_, in_=in_, func=AF.Copy), r=r, w=w)
        else:
            P.op(eng, lambda e: e.tensor_copy(out=out_, in_=in_), r=r, w=w)

    def tt_op(eng, out_, in0, in1, op, r, w):
        P.op(eng, lambda e: e.tensor_tensor(out=out_, in0=in0, in1=in1, op=op), r=r, w=w)

    def pbc(ap_, n):
        return bass.AP(ap_.tensor, ap_.offset, [[0, 128], [1, n]])

    if "p3" in phases:
      with ExitStack() as ph:
        def sbp(name, shape, dt=F32):
            return ph.enter_context(nc.sbuf_tensor(name, list(shape), dt))
        M1 = sbp("gM1s", [128, 128]); M2 = sbp("gM2s", [128, 128]); Mc = sbp("gMcs", [128, 2, 128])
        causneg = sbp("gcausnegs", [128, 128], BF16); strict = sbp("gstricts", [128, 128], BF16)
        rowmask = sbp("growmasks", [128, 2])
        ones_f = sbp("ones_f", [128, 128]); ones_b = sbp("ones_b", [128, 128], BF16)
        onec = sbp("onec", [128, 1]); eps128 = sbp("eps128", [128, 1])
        cw = sbp("cw", [128, 4, 24]); dtb = sbp("dtb", [128, 8]); negA = sbp("negA", [128, 8])
        ggain = sbp("ggain", [128, 128])
        wqkv = sbp("wqkv", [128, 8, 3072], BF16); wba = sbp("wba", [128, 8, 16], BF16)
        halo = sbp("halo", [128, 24, 3], BF16)
        uTg = [sbp("uTg0", [128, 8, 512], BF16)] * 2
        pre = [sbp(f"pre{i}", [128, 515], BF16) for i in range(2)]
        dg = [sbp(f"dg{i}", [128, 4, 128], BF16) for i in range(2)]
        s_all = sbp("s_all", [128, 24, 512], BF16)
        sq = [sbp(f"sq{i}", [128, 512], BF16) for i in range(2)]
        rr = [sbp(f"rr{i}", [128, 512]) for i in range(2)]
        k_tok = sbp("k_tok", [128, 4, 8, 128], BF16); v_tok = sbp("v_tok", [128, 4, 8, 128], BF16)
        eb = sbp("eb", [128, 4, 8]); beta = sbp("beta", [128, 4, 8]); xa = sbp("xa", [128, 4, 8])
        gg = sbp("gg", [128, 4, 16])
        ex = [sbp(f"ex{i}", [128, 64]) for i in range(2)]; cb = sbp("cb", [128, 8]); edm = sbp("edm", [128, 2, 8])
        G2 = sbp("G2", [128, 8, 128])
        Ec = sbp("Ec", [128, 8, 128], BF16); Es = sbp("Es", [128, 8, 128], BF16)
        attn = sbp("attn", [128, 8, 128], BF16); attnT = [sbp(f"attnT{i}", [128, 8, 128], BF16) for i in range(2)]
        Pp = [sbp(f"Pp{i}", [128, 8, 128], BF16) for i in range(2)]
        Qq = [sbp(f"Qq{i}", [128, 8, 128], BF16) for i in range(2)]
        Yy = [sbp(f"Yy{i}", [128, 8, 128], BF16) for i in range(2)]
        egrep = sbp("egrep", [128, 8, 128])
        qgT = [sbp(f"qgT{i}", [128, 8, 128], BF16) for i in range(2)]
        kbg = sbp("kbg", [128, 8, 128], BF16); vb = sbp("vb", [128, 8, 128], BF16)
        kd = [sbp(f"kd{i}", [128, 2, 8, 128], BF16) for i in range(2)]
        wT = [sbp(f"wT{i}", [128, 8, 128], BF16) for i in range(2)]; u32 = [sbp(f"u32_{i}", [128, 8, 128]) for i in range(2)]
        osq_t = sbp("osq_t", [128, 8, 128])
        v_new = sbp("v_new", [128, 8, 128], BF16)
        S32 = sbp("S32", [128, 8, 128]); S_bf = sbp("S_bf", [128, 8, 128], BF16)
        o32 = sbp("o32", [128, 8, 128]); oss = sbp("oss", [128, 8]); orstd = sbp("orstd", [128, 8])
        ynb = sbp("ynb", [128, 8, 128], BF16)
        ynT_st = sbp("ynT_st", [128, 8, 512], BF16)

        for (tile_, src, k) in ((M1, gM1_in, "M1"), (M2, gM2_in, "M2"), (Mc, gMc_in, "Mc"), (causneg, gcausneg_in, "causneg"),
                                (strict, gstrict_in, "strict"), (rowmask, growmask_in, "rowmask")):
            P.dma("sp", tile_[:], src, w=[k])
        for i in range(4):
            P.dma("sp", cw[:, i, :], conv_w[i].rearrange("(c p) -> p c", p=128), w=["cw"])
        P.dma("sp", dtb[:], pbc(dt_bias, 8), w=["dtb"])
        P.dma("sp", negA[:], pbc(a_log, 8), w=["negA0"])
        P.dma("sp", ggain[:], pbc(gdn_norm, 128), w=["ggain"])
        for j in range(6):
            P.dma("pool", wqkv[:, :, j * 512:(j + 1) * 512], w_in_v[:, :, 3072 + j * 512:3072 + (j + 1) * 512],
                  w=[("wqkv", j)])
        P.dma("pool", wba[:], w_in_v[:, :, 7168:7184], w=["wba"])
        P.op("pool", lambda e: e.memset(ones_f[:], 1.0), w=["ones_f"])
        P.op("pool", lambda e: e.memset(ones_b[:], 1.0), w=["ones_b"])
        P.op("pool", lambda e: e.memset(onec[:], 1.0), w=["onec"])
        P.op("pool", lambda e: e.memset(eps128[:], 128.0 * EPS), w=["eps128"])
        P.op("pool", lambda e: e.memset(halo[:], 0.0), w=["halo"])
        P.op("pool", lambda e: e.memset(gg[:], 0.0), w=["gg"])
        P.op("pool", lambda e: e.memset(v_new[:], 0.0), w=[("v_new", 0), ("v_new", 1)])
        P.op("pool", lambda e: e.memset(S32[:], 0.0), w=[("S32", 0), ("S32", 1)])
        P.op("pool", lambda e: e.memset(S_bf[:], 0.0), w=[("S_bf", 0), ("S_bf", 1)])
        P.op("act", lambda e: e.activation(out=negA[:], in_=negA[:], func=AF.Exp), r=["negA0"], w=["negA1"])
        P.op("dve", lambda e: e.tensor_scalar(out=negA[:], in0=negA[:], scalar1=-1.0, scalar2=None, op0=ALU.mult),
             r=["negA1"], w=["negA"])

        rot = [0]

        def nb():
            b = 4 + rot[0] % 4
            rot[0] += 1
            return b

        def hview(bk):
            return banks[bk][:, :].rearrange("p (h d) -> p h d", d=128)

        def group_stage(g):
            ug = uTg[g % 2]
            P.dma("sp", ug[:], uT_scr[:, :, g * 512:(g + 1) * 512], r=[("uT_scr", g)], w=[("uTg", 0)])
            def proj_cc(cc):
                i2 = cc % 2
                bk = nb()
                for kc in range(8):
                    mm(banks[bk][:, :], wqkv[:, kc, cc * 128:(cc + 1) * 128], ug[:, kc, :],
                       r=[("wqkv", cc // 4), ("uTg", 0)], w=[("ps", bk)], start=(kc == 0), stop=(kc == 7))
                ev(pre[i2][:, 3:515], banks[bk][:, :], r=[("ps", bk)], w=[("pre", i2)], eng="dve")
                P.op("pool", lambda e, i2=i2, cc=cc: e.tensor_copy(out=pre[i2][:, 0:3], in_=halo[:, cc, :]),
                     r=["halo", ("pre", i2)], w=[("pre", i2)])
                P.op("pool", lambda e, i2=i2, cc=cc: e.tensor_copy(out=halo[:, cc, :], in_=pre[i2][:, 512:515]),
                     r=[("pre", i2)], w=["halo"])
                tt_op("pool", dg[i2][:], bcast_ap(ident_f[:], [[0, 4], [1, 128]]),
                      bcast_ap(cw[:, 0, cc:cc + 1], [[24, 4], [0, 128]]), ALU.mult, r=["ident_f", "cw"], w=[("dg", i2)])

            def conv_cc(cc):
                i2 = cc % 2
                bk2 = nb()
                for i in range(4):
                    mm(banks[bk2][:, :], dg[i2][:, i, :], pre[i2][:, i:i + 512],
                       r=[("dg", i2), ("pre", i2)], w=[("ps", bk2)], start=(i == 0), stop=(i == 3))
                P.op("act", lambda e, cc=cc, bk2=bk2: e.activation(out=s_all[:, cc, :], in_=banks[bk2][:, :], func=AF.Silu),
                     r=[("ps", bk2)], w=[("s", cc)])

            proj_cc(0)
            for cc in range(24):
                if cc + 1 < 24:
                    proj_cc(cc + 1)
                conv_cc(cc)
            for cc in range(16):
                i2 = cc % 2
                tt_op("pool", sq[i2][:], s_all[:, cc, :], s_all[:, cc, :], ALU.mult, r=[("s", cc)], w=[("sq", i2)])
                bk = nb()
                mm(banks[bk][:, :], ones_b[:], sq[i2][:], r=["ones_b", ("sq", i2)], w=[("ps", bk)])
                if cc < 8:
                    P.op("act", lambda e, i2=i2, bk=bk: e.activation(out=rr[i2][:], in_=banks[bk][:, :], func=AF.Ln,
                                                                      bias=eps128[:], scale=128.0),
                         r=[("ps", bk), "eps128"], w=[("rr", i2)])
                else:
                    P.op("act", lambda e, i2=i2, bk=bk: e.activation(out=rr[i2][:], in_=banks[bk][:, :], func=AF.Ln,
                                                                      bias=epsc[:], scale=1.0),
                         r=[("ps", bk), "epsc"], w=[("rr", i2)])
                P.op("act", lambda e, i2=i2: e.activation(out=rr[i2][:], in_=rr[i2][:], func=AF.Exp, scale=-0.5),
                     r=[("rr", i2)], w=[("rr", i2)])
                tt_op("dve", s_all[:, cc, :], s_all[:, cc, :], rr[i2][:], ALU.mult, r=[("s", cc), ("rr", i2)], w=[("s", cc)])
            for tt in range(4):
                for (base, dst, nm) in ((8, k_tok, "k_tok"), (16, v_tok, "v_tok")):
                    bk = nb()
                    pv = banks[bk][:, :].bitcast(BF16)
                    for h in range(8):
                        tr(pv[:, h * 128:(h + 1) * 128], s_all[:, base + h, tt * 128:(tt + 1) * 128],
                           r=[("s", base + h)], w=[("ps", bk)])
                    ev(dst[:, tt, :, :], pv.rearrange("p (h d) -> p h d", d=128), r=[("ps", bk)], w=[(nm, tt)])
            bk = nb()
            for tt in range(4):
                for kc in range(8):
                    mm(banks[bk][:, tt * 16:(tt + 1) * 16], ug[:, kc, tt * 128:(tt + 1) * 128], wba[:, kc, :],
                       r=[("uTg", 0), "wba"], w=[("ps", bk)], start=(kc == 0), stop=(kc == 7))
            bav = banks[bk][:, 0:64].rearrange("p (t c) -> p t c", c=16)
            P.op("act", lambda e: e.activation(out=eb[:], in_=bav[:, :, 0:8], func=AF.Exp, scale=-1.0),
                 r=[("ps", bk)], w=["eb"])
            P.op("dve", lambda e: e.tensor_scalar(out=eb[:], in0=eb[:], scalar1=1.0, scalar2=None, op0=ALU.add),
                 r=["eb"], w=["eb"])
            P.op("dve", lambda e: e.reciprocal(out=beta[:], in_=eb[:]), r=["eb"], w=["beta"])
            tt_op("dve", xa[:], bav[:, :, 8:16], bcast_ap(dtb[:], [[0, 4], [1, 8]]), ALU.add, r=[("ps", bk), "dtb"], w=["xa"])
            P.op("act", lambda e: e.activation(out=xa[:], in_=xa[:], func=AF.Exp), r=["xa"], w=["xa"])
            P.op("act", lambda e: e.activation(out=xa[:], in_=xa[:], func=AF.Ln, bias=onec[:], scale=1.0),
                 r=["xa", "onec"], w=["xa"])
            tt_op("dve", gg[:, :, 0:8], xa[:], bcast_ap(negA[:], [[0, 4], [1, 8]]), ALU.mult, r=["xa", "negA"], w=["gg"])

        def pair_prep(g, tt, par):
            tsl = slice(tt * 128, (tt + 1) * 128)
            gv = gg[:, tt, 0:8]
            gv16 = gg[:, tt, :]
            bv = beta[:, tt, :]
            bk = nb()
            mm(banks[bk][:, 0:16], M1[:], gv16, r=["M1", "gg"], w=[("ps", bk)])
            mm(banks[bk][:, 16:32], M2[:], gv16, r=["M2", "gg"], w=[("ps", bk)])
            mm(banks[bk][:, 32:48], Mc[:, 0, :], gv16, r=["Mc", "gg"], w=[("ps", bk)])
            mm(banks[bk][:, 48:64], Mc[:, 1, :], gv16, r=["Mc", "gg"], w=[("ps", bk)])
            P.op("act", lambda e, bk=bk: e.activation(out=ex[par][:], in_=banks[bk][:, 0:64], func=AF.Exp), r=[("ps", bk)], w=[("ex", par)])
            tt_op("dve", cb[:], bv, ex[par][:, 0:8], ALU.mult, r=["beta", ("ex", par)], w=["cb"])
            for c in range(2):
                P.op("dve", lambda e, c=c: e.tensor_scalar(out=edm[:, c, :], in0=ex[par][:, 16:24], scalar1=rowmask[:, c:c + 1],
                                                         scalar2=None, op0=ALU.mult),
                     r=[("ex", par), "rowmask"], w=[("edm", c)])
            yield
            tt_op("pool", G2[:], bcast_ap(M1[:], [[0, 8], [1, 128]]), bcast_ap(gv, [[1, 8], [0, 128]]), ALU.mult,
                  r=["M1", "gg"], w=["G2"])
            for hg in range(2):
                bk = nb()
                for hh in range(4):
                    h = hg * 4 + hh
                    mm(banks[bk][:, hh * 128:(hh + 1) * 128], G2[:, h, :], M2[:], r=["G2", "M2"], w=[("ps", bk)],
                       start=True, stop=False)
                    mm(banks[bk][:, hh * 128:(hh + 1) * 128], ident_b[:], causneg[:], r=["ident_b", "causneg"],
                       w=[("ps", bk)], start=False, stop=True)
                P.op("act", lambda e, hg=hg, bk=bk: e.activation(out=Ec[:, hg * 4:(hg + 1) * 4, :], in_=hview(bk), func=AF.Exp),
                     r=[("ps", bk)], w=[("Ec", hg)])
            tt_op("pool", Es[:], Ec[:], bcast_ap(strict[:], [[0, 8], [1, 128]]), ALU.mult,
                  r=[("Ec", 0), ("Ec", 1), "strict"], w=["Es"])
            tt_op("pool", Es[:], Es[:], bcast_ap(bv, [[1, 8], [0, 128]]), ALU.mult, r=["Es", "beta"], w=["Es"])
            for hg in range(2):
                bk = nb()
                for hh in range(4):
                    h = hg * 4 + hh
                    mm(banks[bk][:, hh * 128:(hh + 1) * 128], ones_f[:], G2[:, h, :], r=["G2", "ones_f"], w=[("ps", bk)])
                P.op("act", lambda e, hg=hg, bk=bk: e.activation(out=egrep[:, hg * 4:(hg + 1) * 4, :], in_=hview(bk), func=AF.Exp),
                     r=[("ps", bk)], w=[("egrep", hg)])
            tt_op("pool", qgT[par][:], s_all[:, 0:8, tsl], egrep[:], ALU.mult,
                  r=[("s", c_) for c_ in range(8)] + [("egrep", 0), ("egrep", 1)], w=[("qgT", par)])
            yield
            A_ = Qq[0]
            B_ = Pp[0]
            for hg in range(2):
                bk = nb()
                for hh in range(4):
                    h = hg * 4 + hh
                    mm(banks[bk][:, hh * 128:(hh + 1) * 128], s_all[:, 8 + h, tsl], s_all[:, 8 + h, tsl],
                       r=[("s", 8 + h)], w=[("ps", bk)])
                tt_op("dve", A_[:, hg * 4:(hg + 1) * 4, :], hview(bk), Es[:, hg * 4:(hg + 1) * 4, :], ALU.mult,
                      r=[("ps", bk), "Es"], w=[("Q", 0, hg)])
                yield
            for hg in range(2):
                bk = nb()
                for hh in range(4):
                    h = hg * 4 + hh
                    mm(banks[bk][:, hh * 128:(hh + 1) * 128], s_all[:, h, tsl], s_all[:, 8 + h, tsl],
                       r=[("s", h), ("s", 8 + h)], w=[("ps", bk)])
                tt_op("dve", attn[:, hg * 4:(hg + 1) * 4, :], hview(bk), Ec[:, hg * 4:(hg + 1) * 4, :], ALU.mult,
                      r=[("ps", bk), ("Ec", hg)], w=[("attn", hg)])
            for (src, dst, rk, wk) in ((A_, B_, [("Q", 0, 0), ("Q", 0, 1)], [("P", 0, 0), ("P", 0, 1)]),
                                       (attn, attnT[par], [("attn", 0), ("attn", 1)], [("attnT", par)])):
                bk = nb()
                pv = banks[bk][:, :].bitcast(BF16)
                for h in range(8):
                    tr(pv[:, h * 128:(h + 1) * 128], src[:, h, :], r=rk, w=[("ps", bk)])
                ev(dst[:], pv.rearrange("p (h d) -> p h d", d=128), r=[("ps", bk)], w=wk)
            yield
            tt_op("pool", Yy[0][:], bcast_ap(ident_b[:], [[0, 8], [1, 128]]), B_[:], ALU.subtract,
                  r=["ident_b", ("P", 0, 0), ("P", 0, 1)], w=[("Y", 0, 0), ("Y", 0, 1)])
            cur = 0
            for lvl in range(1, 6):
                nxt = 1 - cur
                for hg in range(2):
                    bk = nb()
                    for hh in range(4):
                        h = hg * 4 + hh
                        mm(banks[bk][:, hh * 128:(hh + 1) * 128], Pp[cur][:, h, :], Qq[cur][:, h, :],
                           r=[("P", cur, hg), ("Q", cur, hg)], w=[("ps", bk)])
                    ev(Qq[nxt][:, hg * 4:(hg + 1) * 4, :], hview(bk), r=[("ps", bk)], w=[("Q", nxt, hg)])
                    if lvl < 5:
                        bk = nb()
                        for hh in range(4):
                            h = hg * 4 + hh
                            mm(banks[bk][:, hh * 128:(hh + 1) * 128], Qq[cur][:, h, :], Pp[cur][:, h, :],
                               r=[("P", cur, hg), ("Q", cur, hg)], w=[("ps", bk)])
                        ev(Pp[nxt][:, hg * 4:(hg + 1) * 4, :], hview(bk), r=[("ps", bk)], w=[("P", nxt, hg)])
                    bk = nb()
                    for hh in range(4):
                        h = hg * 4 + hh
                        mm(banks[bk][:, hh * 128:(hh + 1) * 128], ident_b[:], Yy[cur][:, h, :],
                           r=["ident_b", ("Y", cur, hg)], w=[("ps", bk)], start=True, stop=False)
                        mm(banks[bk][:, hh * 128:(hh + 1) * 128], Qq[nxt][:, h, :], Yy[cur][:, h, :],
                           r=[("Q", nxt, hg), ("Y", cur, hg)], w=[("ps", bk)], start=False, stop=True)
                    ev(Yy[nxt][:, hg * 4:(hg + 1) * 4, :], hview(bk), r=[("ps", bk)], w=[("Y", nxt, hg)])
                cur = nxt
                yield
            XT = Yy[cur]
            xk = lambda hg: ("Y", cur, hg)
            yield
            tt_op("pool", kbg[:], k_tok[:, tt, :, :], bcast_ap(cb[:], [[1, 8], [0, 128]]), ALU.mult,
                  r=[("k_tok", tt), "cb"], w=["kbg"])
            tt_op("pool", vb[:], v_tok[:, tt, :, :], bcast_ap(bv, [[1, 8], [0, 128]]), ALU.mult,
                  r=[("v_tok", tt), "beta"], w=["vb"])
            for c in range(2):
                tt_op("pool", kd[par][:, c, :, :], k_tok[:, tt, :, :], bcast_ap(edm[:, c, :], [[1, 8], [0, 128]]), ALU.mult,
                      r=[("k_tok", tt), ("edm", c)], w=[("kd", par, c)])
            for hg in range(2):
                bk = nb()
                for hh in range(4):
                    h = hg * 4 + hh
                    mm(banks[bk][:, hh * 128:(hh + 1) * 128], kbg[:, h, :], XT[:, h, :], r=["kbg", xk(hg)], w=[("ps", bk)])
                ev(wT[par][:, hg * 4:(hg + 1) * 4, :], hview(bk), r=[("ps", bk)], w=[("wT", par, hg)])
                bk = nb()
                for hh in range(4):
                    h = hg * 4 + hh
                    mm(banks[bk][:, hh * 128:(hh + 1) * 128], XT[:, h, :], vb[:, h, :], r=["vb", xk(hg)], w=[("ps", bk)])
                ev(u32[par][:, hg * 4:(hg + 1) * 4, :], hview(bk), r=[("ps", bk)], w=[("u32", par, hg)])
                yield
        def pair_rec(g, tt, par):
            tsl = slice(tt * 128, (tt + 1) * 128)
            for c in range(2):
                rs = slice(c * 64, (c + 1) * 64)
                for hg in range(2):
                    hs = slice(hg * 4, (hg + 1) * 4)
                    bw = hg
                    for hh in range(4):
                        h = hg * 4 + hh
                        mm(banks[bw][:, hh * 128:(hh + 1) * 128], wT[par][:, h, :], S_bf[:, h, :],
                           r=[("wT", par, hg), ("S_bf", hg)], w=[("ps", bw)])
                    P.op("dve", lambda e, rs=rs, hs=hs, bw=bw: e.tensor_tensor(
                        out=v_new[rs, hs, :], in0=u32[par][rs, hs, :], in1=hview(bw)[rs, :, :], op=ALU.subtract),
                        r=[("u32", par, hg), ("ps", bw)], w=[("v_new", hg)])
                for hg in range(2):
                    hs = slice(hg * 4, (hg + 1) * 4)
                    bo = 2 + hg
                    for hh in range(4):
                        h = hg * 4 + hh
                        mm(banks[bo][:, hh * 128:(hh + 1) * 128], qgT[par][:, h, :], S_bf[:, h, :],
                           r=[("qgT", par), ("S_bf", hg)], w=[("ps", bo)], start=True, stop=False)
                        mm(banks[bo][:, hh * 128:(hh + 1) * 128], attnT[par][:, h, :], v_new[:, h, :],
                           r=[("attnT", par), ("v_new", hg)], w=[("ps", bo)], start=False, stop=True)
                    P.op("act", lambda e, rs=rs, hs=hs, bo=bo: e.activation(out=o32[rs, hs, :], in_=hview(bo)[rs, :, :], func=AF.Copy),
                         r=[("ps", bo)], w=[("o32", hg)])
                    bd = nb()
                    for hh in range(4):
                        h = hg * 4 + hh
                        mm(banks[bd][:, hh * 128:(hh + 1) * 128], kd[par][:, c, h, :], v_new[:, h, :],
                           r=[("kd", par, c), ("v_new", hg)], w=[("ps", bd)])
                    tt_op("pool", S32[:, hs, :], S32[:, hs, :], bcast_ap(ex[par][:, 32 + 16 * c + hg * 4:32 + 16 * c + hg * 4 + 1],
                                                                       [[1, 4], [0, 128]]), ALU.mult,
                          r=[("S32", hg), ("ex", par)], w=[("S32", hg)])
                    tt_op("dve", S32[:, hs, :], S32[:, hs, :], hview(bd), ALU.add, r=[("S32", hg), ("ps", bd)], w=[("S32", hg)])
                    P.op("act", lambda e, hs=hs: e.activation(out=S_bf[:, hs, :], in_=S32[:, hs, :], func=AF.Copy),
                         r=[("S32", hg)], w=[("S_bf", hg)])
                    yield
            osq = osq_t
            tt_op("pool", osq[:], o32[:], o32[:], ALU.mult, r=[("o32", 0), ("o32", 1)], w=["osq"])
            P.op("dve", lambda e: e.tensor_reduce(out=oss[:], in_=osq[:], axis=AX.X, op=ALU.add), r=["osq"], w=["oss"])
            P.op("act", lambda e: e.activation(out=orstd[:], in_=oss[:], func=AF.Ln, bias=epsc[:], scale=1.0 / HD),
                 r=["oss", "epsc"], w=["orstd"])
            P.op("act", lambda e: e.activation(out=orstd[:], in_=orstd[:], func=AF.Exp, scale=-0.5), r=["orstd"], w=["orstd"])
            tt_op("dve", o32[:], o32[:], bcast_ap(orstd[:], [[1, 8], [0, 128]]), ALU.mult,
                  r=[("o32", 0), ("o32", 1), "orstd"], w=[("o32", 0), ("o32", 1)])
            tt_op("pool", ynb[:], o32[:], bcast_ap(ggain[:], [[0, 8], [1, 128]]), ALU.mult,
                  r=[("o32", 0), ("o32", 1), "ggain"], w=["ynb"])
            bk = nb()
            pv = banks[bk][:, :].bitcast(BF16)
            for h in range(8):
                tr(pv[:, h * 128:(h + 1) * 128], ynb[:, h, :], r=["ynb"], w=[("ps", bk)])
            ev(ynT_st[:, :, tsl], pv.rearrange("p (h d) -> p h d", d=128), r=[("ps", bk)], w=[("ynT_st", tt)])
            if "gdn_dbg" in debug and g == 0 and tt == 0:
                for (nm, tl, dt_, keys) in ((("ex", par), ex, F32, [("ex", par)]), ("S32", S32, F32, [("S32", 0), ("S32", 1)]),
                                            ("o32", o32, F32, [("o32", 0), ("o32", 1)]), ("wT", wT, BF16, [("wT", 0), ("wT", 1)]),
                                            ("u32", u32, F32, [("u32", 0), ("u32", 1)]), ("gg", gg, F32, ["gg"]),
                                            ("beta", beta, F32, ["beta"]), ("kd", kd, BF16, [("kd", 0), ("kd", 1)]),
                                            ("k_tok", k_tok, BF16, [("k_tok", 0)]), ("v_new", v_new, BF16, [("v_new", 0), ("v_new", 1)])):
                    dd = nc.dram_tensor("dbg_" + nm, list(tl.shape), dt_, kind="ExternalOutput").ap()
                    P.dma("sp", dd, tl[:], r=keys)

            yield

        def interleave(gp, gr):
            done_p = gp is None
            done_r = gr is None
            while not (done_p and done_r):
                if not done_r:
                    try:
                        next(gr)
                    except StopIteration:
                        done_r = True
                if not done_p:
                    try:
                        next(gp)
                    except StopIteration:
                        done_p = True

        pairs = [(g, tt) for g in range(NG) for tt in range(4)]
        group_stage(0)
        interleave(pair_prep(0, 0, 0), None)
        for pi, (g, tt) in enumerate(pairs):
            nxt_gen = None
            if pi + 1 < len(pairs):
                g2, tt2 = pairs[pi + 1]
                if tt2 == 0:
                    group_stage(g2)
                nxt_gen = pair_prep(g2, tt2, (pi + 1) % 2)
            interleave(nxt_gen, pair_rec(g, tt, pi % 2))
            if tt == 3:
                P.dma("sp", ynT_scr[:, :, g * 512:(g + 1) * 512], ynT_st[:], r=[("ynT_st", t_) for t_ in range(4)],
                      w=[("ynT_scr", g)])
        P.emit()

    def kcv(w_ap):
        return w_ap.rearrange("(c p) n -> p c n", p=128)

    def rms_rstd(ssv, outv, nfeat, keyr, keyw):
        P.op("act", lambda e: e.activation(out=outv, in_=ssv, func=AF.Ln, bias=epsc[:], scale=1.0 / nfeat),
             r=list(keyr) + ["epsc"], w=keyw)
        P.op("act", lambda e: e.activation(out=outv, in_=outv, func=AF.Exp, scale=-0.5), r=keyw, w=keyw)

    if "p4" in phases:
      with ExitStack() as ph:
        def sbp(name, shape, dt=F32):
            return ph.enter_context(nc.sbuf_tensor(name, list(shape), dt))
        Wa = sbp("Wa", [128, 8, D], BF16); Wb = sbp("Wb", [128, 8, D], BF16); Wo = sbp("Wo", [128, 8, D], BF16)
        Wga = sbp("Wga", [128, 8, D], BF16); Wgb = sbp("Wgb", [128, 8, D], BF16); Wz = sbp("Wz", [128, 8, D], BF16)
        gpm = sbp("gpm", [128, D])
        uTg = [sbp(f"a_uTg{i}", [128, 8, 512], BF16) for i in range(2)]
        yaTg = [sbp("a_yaTg0", [128, 8, 512], BF16)] * 2
        ynTg = [sbp("a_ynTg0", [128, 8, 512], BF16)] * 2
        ybT = sbp("a_ybT", [128, 8, 512], BF16)
        mgT = sbp("a_mgT", [128, 8, 512], BF16)
        zs = [sbp(f"a_zs{i}", [128, 512], BF16) for i in range(2)]
        sga = [sbp(f"a_sga{i}", [128, 512]) for i in range(2)]
        sgb = [sbp(f"a_sgb{i}", [128, 512]) for i in range(2)]
        m1 = [sbp("a_m10", [128, 512])] * 2
        m2 = [sbp("a_m20", [128, 512])] * 2
        o_sb = [sbp(f"a_osb{i}", [128, D]) for i in range(2)]
        xtl = [sbp(f"a_x{i}", [128, D]) for i in range(2)]
        junk4 = sbp("a_junk", [128, 512])
        ss4 = sbp("a_ss", [128, 4]); rs4 = sbp("a_rs", [128, 2]); sst = sbp("a_sst", [128, 2])
        Wp = sbp("Wp", [128, 2, D], BF16); gpl = sbp("gpl", [128, D])
        pt = [sbp(f"a_pt{i}", [128, PLE]) for i in range(2)]
        ptb = sbp("a_ptb", [128, PLE], BF16); pT = sbp("a_pT", [128, 2, 128], BF16)
        e_sb = [sbp("a_esb0", [128, D])] * 2
        P.dma("pool", Wp[:], kcv(w_ple), w=["Wp"])
        P.dma("sp", gpl[:], pbc(ln_ple, D), w=["gpl"])
        for j in range(2):
            sl = slice(j * 512, (j + 1) * 512)
            P.dma("pool", Wa[:, :, sl], kcv(w_proj_a)[:, :, sl], w=[("Wa", j)])
            P.dma("pool", Wb[:, :, sl], kcv(w_proj_b)[:, :, sl], w=[("Wb", j)])
            P.dma("pool", Wo[:, :, sl], kcv(w_out)[:, :, sl], w=[("Wo", j)])
            P.dma("pool", Wga[:, :, sl], w_in_v[:, :, 7184 + j * 512:7184 + (j + 1) * 512], w=[("Wga", j)])
            P.dma("pool", Wgb[:, :, sl], w_in_v[:, :, 8208 + j * 512:8208 + (j + 1) * 512], w=[("Wgb", j)])
            P.dma("pool", Wz[:, :, sl], w_in_v[:, :, 6144 + j * 512:6144 + (j + 1) * 512], w=[("Wz", j)])
        P.dma("sp", gpm[:], pbc(ln_post_mix, D), w=["gpm"])
        rotA = [0]
        pend_a = []

        def nba():
            b = rotA[0] % 8
            rotA[0] += 1
            return b

        for g in range(NG):
            i2 = g % 2
            gs = slice(g * 512, (g + 1) * 512)
            P.dma("sp", uTg[i2][:], uT_scr[:, :, gs], r=[("uT_scr", g)], w=[("a_uTg", i2)])
            P.dma("sp", yaTg[i2][:], yaT_scr.rearrange("h p s -> p h s")[:, :, gs], r=[("yaT_scr", h) for h in range(NH)],
                  w=[("a_yaTg", 0)])
            P.dma("sp", ynTg[i2][:], ynT_scr[:, :, gs], r=[("ynT_scr", g)], w=[("a_ynTg", 0)])
            for fc in range(8):
                j2 = fc % 2
                bk = nba()
                for kc in range(8):
                    mm(banks[bk][:, :], Wz[:, kc, fc * 128:(fc + 1) * 128], uTg[i2][:, kc, :],
                       r=[("Wz", fc // 4), ("a_uTg", i2)], w=[("ps", bk)], start=(kc == 0), stop=(kc == 7))
                P.op("act", lambda e, j2=j2, bk=bk: e.activation(out=zs[j2][:], in_=banks[bk][:, :], func=AF.Silu),
                     r=[("ps", bk)], w=[("a_zs", j2)])
                tt_op("dve", ybT[:, fc, :], ynTg[i2][:, fc, :], zs[j2][:], ALU.mult, r=[("a_ynTg", 0), ("a_zs", j2)],
                      w=[("a_ybT", fc)])
            for fo in range(8):
                j2 = fo % 2
                fsl = slice(fo * 128, (fo + 1) * 128)
                bA, bB, bC, bD = nba(), nba(), nba(), nba()
                for (bk, W, wn, src, sk) in ((bA, Wa, "Wa", yaTg[i2], [("a_yaTg", 0)]),
                                             (bB, Wb, "Wb", ybT, [("a_ybT", c_) for c_ in range(8)]),
                                             (bC, Wga, "Wga", uTg[i2], [("a_uTg", i2)]),
                                             (bD, Wgb, "Wgb", uTg[i2], [("a_uTg", i2)])):
                    for kc in range(8):
                        mm(banks[bk][:, :], W[:, kc, fsl], src[:, kc, :], r=[(wn, fo // 4)] + sk, w=[("ps", bk)],
                           start=(kc == 0), stop=(kc == 7))
                P.op("act", lambda e, j2=j2, bC=bC: e.activation(out=sga[j2][:], in_=banks[bC][:, :], func=AF.Sigmoid),
                     r=[("ps", bC)], w=[("a_sga", j2)])
                P.op("act", lambda e, j2=j2, bD=bD: e.activation(out=sgb[j2][:], in_=banks[bD][:, :], func=AF.Sigmoid),
                     r=[("ps", bD)], w=[("a_sgb", j2)])
                tt_op("dve", m1[j2][:], banks[bA][:, :], sga[j2][:], ALU.mult, r=[("ps", bA), ("a_sga", j2)], w=[("a_m1", 0)])
                tt_op("dve", m2[j2][:], banks[bB][:, :], sgb[j2][:], ALU.mult, r=[("ps", bB), ("a_sgb", j2)], w=[("a_m2", 0)])
                tt_op("pool", mgT[:, fo, :], m1[j2][:], m2[j2][:], ALU.add, r=[("a_m1", 0), ("a_m2", 0)], w=[("a_mgT", fo)])
            for tt in range(4):
                t = g * 4 + tt
                j2 = t % 2
                P.dma("sp", xtl[j2][:], x[t * 128:(t + 1) * 128, :], w=[("a_x", j2)])
                P.dma("sp", pt[j2][:], p_in[t * 128:(t + 1) * 128, :], w=[("a_pt", j2)])
                while pend_a:
                    pend_a.pop(0)()
                P.op("dve", lambda e, j2=j2: e.tensor_copy(out=ptb[:], in_=pt[j2][:]), r=[("a_pt", j2)], w=["a_ptb"])
                for hf in range(2):
                    bk = nba()
                    for kc in range(8):
                        mm(banks[bk][:, :], mgT[:, kc, tt * 128:(tt + 1) * 128], Wo[:, kc, hf * 512:(hf + 1) * 512],
                           r=[("a_mgT", kc), ("Wo", hf)], w=[("ps", bk)], start=(kc == 0), stop=(kc == 7))
                    ev(o_sb[j2][:, hf * 512:(hf + 1) * 512], banks[bk][:, :], r=[("ps", bk)], w=[("a_osb", j2, hf)], eng="dve")
                    P.op("act", lambda e, bk=bk, hf=hf: e.activation(out=junk4[:], in_=banks[bk][:, :], func=AF.Square,
                                                                      accum_out=ss4[:, hf:hf + 1]),
                         r=[("ps", bk)], w=[("a_ss", hf), "a_junk"])
                bk = nba()
                pv = banks[bk][:, 0:128].bitcast(BF16)
                for c in range(2):
                    tr(pv[:, c * 128:(c + 1) * 128], ptb[:, c * 128:(c + 1) * 128], r=["a_ptb"], w=[("ps", bk)])
                ev(pT[:], pv.rearrange("p (c j) -> p c j", c=2), r=[("ps", bk)], w=["a_pT"], eng="dve")
                for hf in range(2):
                    bk = nba()
                    for kc in range(2):
                        mm(banks[bk][:, :], pT[:, kc, :], Wp[:, kc, hf * 512:(hf + 1) * 512], r=["a_pT", "Wp"],
                           w=[("ps", bk)], start=(kc == 0), stop=(kc == 1))
                    ev(e_sb[j2][:, hf * 512:(hf + 1) * 512], banks[bk][:, :], r=[("ps", bk)], w=[("a_esb", 0, hf)], eng="dve")
                    P.op("act", lambda e, bk=bk, hf=hf: e.activation(out=junk4[:], in_=banks[bk][:, :], func=AF.Square,
                                                                      accum_out=ss4[:, 2 + hf:3 + hf]),
                         r=[("ps", bk)], w=[("a_ss", 2 + hf), "a_junk"])
                tt_op("dve", sst[:, 0:1], ss4[:, 0:1], ss4[:, 1:2], ALU.add, r=[("a_ss", 0), ("a_ss", 1)], w=["a_sst"])
                rms_rstd(sst[:, 0:1], rs4[:, 0:1], D, ["a_sst"], ["a_rs"])
                P.op("dve", lambda e, j2=j2: e.scalar_tensor_tensor(out=o_sb[j2][:], in0=o_sb[j2][:], scalar=rs4[:, 0:1],
                                                                    in1=gpm[:], op0=ALU.mult, op1=ALU.mult),
                     r=[("a_osb", j2, 0), ("a_osb", j2, 1), "a_rs", "gpm"], w=[("a_osb", j2, 0), ("a_osb", j2, 1)])
                tt_op("pool", o_sb[j2][:], o_sb[j2][:], xtl[j2][:], ALU.add,
                      r=[("a_osb", j2, 0), ("a_osb", j2, 1), ("a_x", j2)], w=[("a_osb", j2, 0), ("a_osb", j2, 1)])
                pend_a.append(lambda t=t, j2=j2: P.dma("sp", h1_scr[t * 128:(t + 1) * 128, :], o_sb[j2][:],
                                                      r=[("a_osb", j2, 0), ("a_osb", j2, 1)], w=[("h1_scr", t)]))
                tt_op("dve", sst[:, 1:2], ss4[:, 2:3], ss4[:, 3:4], ALU.add, r=[("a_ss", 2), ("a_ss", 3)], w=["a_sst1"])
                rms_rstd(sst[:, 1:2], rs4[:, 1:2], D, ["a_sst1"], ["a_rs1"])
                P.op("dve", lambda e, j2=j2: e.scalar_tensor_tensor(out=e_sb[j2][:], in0=e_sb[j2][:], scalar=rs4[:, 1:2],
                                                                    in1=gpl[:], op0=ALU.mult, op1=ALU.mult),
                     r=[("a_esb", 0, 0), ("a_esb", 0, 1), "a_rs1", "gpl"], w=[("a_esb", 0, 0), ("a_esb", 0, 1)])
                pend_a.append(lambda t=t, j2=j2: P.dma("sp", e_scr[t * 128:(t + 1) * 128, :], e_sb[j2][:],
                                                      r=[("a_esb", 0, 0), ("a_esb", 0, 1)], w=[("e_scr", t)]))
        while pend_a:
            pend_a.pop(0)()
        P.emit()

    if "p4" in phases:
      with ExitStack() as ph:
        def sbp(name, shape, dt=F32):
            return ph.enter_context(nc.sbuf_tensor(name, list(shape), dt))
        GT = 256
        Wg = sbp("Wg", [128, 8, DFF], BF16); Wu = sbp("Wu", [128, 8, DFF], BF16); Wd = sbp("Wd", [128, 22, D], BF16)
        Wpg = sbp("Wpg", [128, 8, D], BF16)
        gff = sbp("gff", [128, D]); gT2 = sbp("gT2", [128, 8])
        fT = [sbp(f"b_fT{i}", [128, 8, GT], BF16) for i in range(2)]
        hT = sbp("b_hT", [128, 22, GT], BF16)
        h1t = [sbp(f"b_h1t{i}", [128, D]) for i in range(4)]
        fbf = sbp("b_fbf", [128, D], BF16)
        sgf = [sbp(f"b_sgf{i}", [128, GT], BF16) for i in range(2)]
        ffs = [sbp(f"b_ffs{i}", [128, D]) for i in range(2)]
        h2b = sbp("b_h2b", [128, D], BF16)
        h2T = sbp("b_h2T", [128, 8, 128], BF16)
        sgp = sbp("b_sgp", [128, D], BF16)
        junk5 = sbp("b_junk", [128, D], BF16)
        ssb = sbp("b_ss", [128, 4]); rsb = sbp("b_rs", [128, 4]); ssq = sbp("b_ssq", [128, 4])
        for kc in range(8):
            P.dma("pool", Wg[:, kc, :], kcv(w_ffn_gate)[:, kc, :], w=[("Wg", kc)])
            P.dma("pool", Wu[:, kc, :], kcv(w_ffn_up)[:, kc, :], w=[("Wu", kc)])
        for c2 in range(11):
            P.dma("pool", Wd[:, 2 * c2:2 * c2 + 2, :], kcv(w_ffn_down)[:, 2 * c2:2 * c2 + 2, :], w=[("Wd", c2)])
        for j in range(2):
            sl = slice(j * 512, (j + 1) * 512)
            P.dma("pool", Wpg[:, :, sl], kcv(w_ple_gate)[:, :, sl], w=[("Wpg", j)])
        P.dma("sp", gff[:], pbc(ln_post_ffn, D), w=["gff"])
        P.dma("sp", gT2[:], ln_pre_ffn.rearrange("(c p) -> p c", p=128), w=["gT2"])
        rotB = [0]

        def nbb():
            b = rotB[0] % 8
            rotB[0] += 1
            return b

        NG2 = S // GT
        pend_b = []

        def prenorm(g):
            for tt in range(2):
                t = g * 2 + tt
                j4 = t % 4
                P.dma("sp", h1t[j4][:], h1_scr[t * 128:(t + 1) * 128, :], r=[("h1_scr", t)], w=[("b_h1t", j4)])
                P.op("act", lambda e, j4=j4: e.activation(out=junk5[:], in_=h1t[j4][:], func=AF.Square, accum_out=ssb[:, 0:1]),
                     r=[("b_h1t", j4)], w=[("b_ss", 0), "b_junk"])
                rms_rstd(ssb[:, 0:1], rsb[:, 0:1], D, [("b_ss", 0)], [("b_rs", 0)])
                P.op("dve", lambda e, j4=j4: e.tensor_scalar(out=fbf[:], in0=h1t[j4][:], scalar1=rsb[:, 0:1], scalar2=None,
                                                           op0=ALU.mult),
                     r=[("b_h1t", j4), ("b_rs", 0)], w=["b_fbf"])
                bk = nbb()
                pv = banks[bk][:, :].bitcast(BF16)
                for c in range(8):
                    tr(pv[:, c * 128:(c + 1) * 128], fbf[:, c * 128:(c + 1) * 128], r=["b_fbf"], w=[("ps", bk)])
                tt_op("dve", fT[g % 2][:, :, tt * 128:(tt + 1) * 128], pv.rearrange("p (c j) -> p c j", c=8),
                      bcast_ap(gT2[:], [[1, 8], [0, 128]]), ALU.mult, r=[("ps", bk), "gT2"], w=[("b_fT", g % 2, tt)])

        def gateup(g):
            fTg = fT[g % 2]
            fT_keys = [("b_fT", g % 2, tt) for tt in range(2)]
            for fc in range(22):
                j2 = fc % 2
                fsl = slice(fc * 128, (fc + 1) * 128)
                bG, bU = nbb(), nbb()
                for kc in range(8):
                    mm(banks[bG][:, 0:GT], Wg[:, kc, fsl], fTg[:, kc, :], r=[("Wg", kc)] + fT_keys, w=[("ps", bG)],
                       start=(kc == 0), stop=(kc == 7))
                for kc in range(8):
                    mm(banks[bU][:, 0:GT], Wu[:, kc, fsl], fTg[:, kc, :], r=[("Wu", kc)] + fT_keys, w=[("ps", bU)],
                       start=(kc == 0), stop=(kc == 7))
                P.op("act", lambda e, j2=j2, bG=bG: e.activation(out=sgf[j2][:], in_=banks[bG][:, 0:GT], func=AF.Silu),
                     r=[("ps", bG)], w=[("b_sgf", j2)])
                tt_op("dve", hT[:, fc, :], sgf[j2][:], banks[bU][:, 0:GT], ALU.mult, r=[("b_sgf", j2), ("ps", bU)],
                      w=[("b_hT", fc)])

        def down(g, tt):
            t = g * 2 + tt
            fb = ffs[tt]
            for hf in range(2):
                bk = nbb()
                for fc in range(22):
                    mm(banks[bk][:, :], hT[:, fc, tt * 128:(tt + 1) * 128], Wd[:, fc, hf * 512:(hf + 1) * 512],
                       r=[("b_hT", fc), ("Wd", fc // 2)], w=[("ps", bk)], start=(fc == 0), stop=(fc == 21))
                ev(fb[:, hf * 512:(hf + 1) * 512], banks[bk][:, :], r=[("ps", bk)], w=[("b_ffs", tt, hf)], eng="dve")
                P.op("act", lambda e, bk=bk, hf=hf, tt=tt: e.activation(out=junk5[:, 0:512], in_=banks[bk][:, :], func=AF.Square,
                                                                         accum_out=ssq[:, 2 * tt + hf:2 * tt + hf + 1]),
                     r=[("ps", bk)], w=[("b_ssq", tt, hf), "b_junk"])

        def tail(g, tt):
            t = g * 2 + tt
            j4 = t % 4
            fb = ffs[tt]
            fk = [("b_ffs", tt, 0), ("b_ffs", tt, 1)]
            tt_op("dve", ssb[:, 1 + tt:2 + tt], ssq[:, 2 * tt:2 * tt + 1], ssq[:, 2 * tt + 1:2 * tt + 2], ALU.add,
                  r=[("b_ssq", tt, 0), ("b_ssq", tt, 1)], w=[("b_ss", 1 + tt)])
            rms_rstd(ssb[:, 1 + tt:2 + tt], rsb[:, 1 + tt:2 + tt], D, [("b_ss", 1 + tt)], [("b_rs", 1 + tt)])
            P.op("dve", lambda e: e.scalar_tensor_tensor(out=fb[:], in0=fb[:], scalar=rsb[:, 1 + tt:2 + tt], in1=gff[:],
                                                         op0=ALU.mult, op1=ALU.mult),
                 r=fk + [("b_rs", 1 + tt), "gff"], w=fk)
            tt_op("pool", h1t[j4][:], h1t[j4][:], fb[:], ALU.add, r=[("b_h1t", j4)] + fk, w=[("b_h1t", j4)])
            P.dma("sp", fb[:], e_scr[t * 128:(t + 1) * 128, :], r=[("e_scr", t)], w=fk)
            P.op("act", lambda e: e.activation(out=h2b[:], in_=h1t[j4][:], func=AF.Copy), r=[("b_h1t", j4)], w=["b_h2b"])
            bk = nbb()
            pv = banks[bk][:, :].bitcast(BF16)
            for c in range(8):
                tr(pv[:, c * 128:(c + 1) * 128], h2b[:, c * 128:(c + 1) * 128], r=["b_h2b"], w=[("ps", bk)])
            ev(h2T[:], pv.rearrange("p (c j) -> p c j", c=8), r=[("ps", bk)], w=["b_h2T"], eng="dve")
            for hf in range(2):
                bk = nbb()
                for kc in range(8):
                    mm(banks[bk][:, :], h2T[:, kc, :], Wpg[:, kc, hf * 512:(hf + 1) * 512], r=["b_h2T", ("Wpg", hf)],
                       w=[("ps", bk)], start=(kc == 0), stop=(kc == 7))
                P.op("act", lambda e, bk=bk, hf=hf: e.activation(out=sgp[:, hf * 512:(hf + 1) * 512], in_=banks[bk][:, :],
                                                                  func=AF.Sigmoid),
                     r=[("ps", bk)], w=[("b_sgp", hf)])
            tt_op("pool", fb[:], fb[:], sgp[:], ALU.mult, r=fk + [("b_sgp", 0), ("b_sgp", 1)], w=fk)
            tt_op("pool", fb[:], fb[:], h1t[j4][:], ALU.add, r=fk + [("b_h1t", j4)], w=fk)
            P.dma("sp", out[t * 128:(t + 1) * 128, :], fb[:], r=fk, w=[("out", t)])

        prenorm(0)
        for g in range(NG2):
            gateup(g)
            if g + 1 < NG2:
                prenorm(g + 1)
            down(g, 0)
            down(g, 1)
            tail(g, 0)
            tail(g, 1)
        P.finish()
        P.emit()
    else:
        P.finish()
        P.emit()
    es.close()
    return nc


_WNAMES = ("ln_pre_mix", "w_in", "conv_w", "a_log", "dt_bias", "gdn_norm", "w_proj_a", "w_proj_b", "w_out",
           "ln_post_mix", "ln_pre_ffn", "w_ffn_gate", "w_ffn_up", "w_ffn_down", "ln_post_ffn", "w_ple", "ln_ple",
           "w_ple_gate")


def make_in_maps(inputs, S, n_cores):
    consts = host_consts(S)
    shared = {k: np.ascontiguousarray(np.asarray(inputs[k], dtype=np.float32)[0]) for k in _WNAMES}
    x = np.asarray(inputs["x"], dtype=np.float32)
    p = np.asarray(inputs["p"], dtype=np.float32)
    maps = []
    for i in range(n_cores):
        m = dict(shared)
        m.update(consts)
        m["x"] = np.ascontiguousarray(x[i])
        m["p"] = np.ascontiguousarray(p[0, i])
        maps.append(m)
    return maps


def kernel(**inputs):
    x = np.asarray(inputs["x"])
    B, S, _ = x.shape
    nc = build(S)
    in_maps = make_in_maps(inputs, S, B)
    res = run_bass_kernel_spmd(nc, in_maps, core_ids=list(range(B)))
    return np.stack([np.asarray(r["out"], dtype=np.float32) for r in res.results], axis=0)
```
